# Optimizing a Trainium2 kernel written in Bass

```python
import math
import jax, jax.numpy as jnp
from jax import lax
import numpy as np

D_MODEL = 2048
BATCH = 16
SEQ = 2048
DEPTH = 1
DEC_BATCH = 8
DEC_SEQ = 16
PAST_LEN = 1024

CHUNK = 64
Q_BLOCK = 128
H_A = 8
DH_A = 64
DK_A = 2 * DH_A
DV_A = 128
H_B = 8
DH_B = 128
N_PREV_CHUNKS = 8
BAND_PAST = N_PREV_CHUNKS * CHUNK
BAND_LEN = BAND_PAST + CHUNK
MAX_REL = 128
H_C = 4
DH_C = 256
N_MEM = 256
N_BRANCH = 3
W_A = H_A * DV_A
W_B = H_B * DH_B
W_C = H_C * DH_C
D_FF = ((8 * D_MODEL + 3 * 256 - 1) // (3 * 256)) * 256
IN_SIZES = [H_A * DK_A, H_A * DK_A, W_A, W_B, W_B, W_B, W_C, N_BRANCH * D_MODEL]
N_IN = sum(IN_SIZES)
SPLIT_IDX = [int(i) for i in np.cumsum(IN_SIZES)[:-1]]
EPS = 1e-6
NEG_INF = -1e30

kernel_name = "gated_hybrid_streaming_encoder_step"


def rms_norm(x, g):
    xf = x.astype(jnp.float32)
    y = xf * lax.rsqrt(jnp.mean(xf * xf, axis=-1, keepdims=True) + EPS) * g.astype(jnp.float32)
    return y.astype(x.dtype)


def head_norm(o, g, lam_init):
    of = o.astype(jnp.float32)
    y = of * lax.rsqrt(jnp.mean(of * of, axis=-1, keepdims=True) + EPS) * g.astype(jnp.float32) * (1.0 - lam_init)
    return y.astype(o.dtype)


def alibi_slopes():
    return jnp.asarray([2.0 ** (-8.0 * (h + 1) / H_A) for h in range(H_A)], dtype=jnp.float32)


def diff_lambda(lq1, lk1, lq2, lk2, lam_init):
    f = lambda a: a.astype(jnp.float32)
    return jnp.exp(jnp.sum(f(lq1) * f(lk1))) - jnp.exp(jnp.sum(f(lq2) * f(lk2))) + lam_init


def diff_attend(q, k, v, qpos, kpos, lam):
    q1, q2 = jnp.split(q, 2, axis=-1)
    k1, k2 = jnp.split(k, 2, axis=-1)
    dist = jnp.abs(qpos[:, None] - kpos[None, :]).astype(jnp.float32)
    bias = -alibi_slopes()[:, None, None] * dist[None]
    visible = (kpos[None, :] // CHUNK) <= (qpos[:, None] // CHUNK)
    scale = DH_A ** -0.5

    def probs(qa, ka):
        s = jnp.einsum("bqhd,bkhd->bhqk", qa, ka).astype(jnp.float32) * scale + bias[None]
        return jax.nn.softmax(jnp.where(visible[None, None], s, NEG_INF), axis=-1)

    a = probs(q1, k1) - lam * probs(q2, k2)
    return jnp.einsum("bhqk,bkhd->bqhd", a.astype(v.dtype), v)


def diff_attention_prompt(q, k, v, lam):
    B, S = q.shape[0], q.shape[1]
    nb = S // Q_BLOCK
    kpos = jnp.arange(S)
    qb = q.reshape(B, nb, Q_BLOCK, H_A, DK_A).transpose(1, 0, 2, 3, 4)

    def step(args):
        qblk, i = args
        qpos = i * Q_BLOCK + jnp.arange(Q_BLOCK)
        return diff_attend(qblk, k, v, qpos, kpos, lam)

    o = lax.map(step, (qb, jnp.arange(nb)))
    return o.transpose(1, 0, 2, 3, 4).reshape(B, S, H_A, DV_A)


def band_attend(q, k, v, qpos, kpos, rel_bias):
    s = jnp.einsum("bqhd,bkhd->bhqk", q, k).astype(jnp.float32) * (DH_B ** -0.5)
    rel = jnp.clip(kpos[None, :] - qpos[:, None], -MAX_REL, MAX_REL) + MAX_REL
    s = s + jnp.take(rel_bias, rel, axis=1).astype(jnp.float32)[None]
    qch = qpos[:, None] // CHUNK
    kch = kpos[None, :] // CHUNK
    valid = (kpos[None, :] >= 0) & (kch <= qch) & (kch >= qch - N_PREV_CHUNKS)
    p = jax.nn.softmax(jnp.where(valid[None, None], s, NEG_INF), axis=-1)
    return jnp.einsum("bhqk,bkhd->bqhd", p.astype(v.dtype), v)


def band_attention_prompt(q, k, v, rel_bias):
    B, S = q.shape[0], q.shape[1]
    nc = S // CHUNK
    pad = ((0, 0), (BAND_PAST, 0), (0, 0), (0, 0))
    kpad = jnp.pad(k, pad)
    vpad = jnp.pad(v, pad)
    qc = q.reshape(B, nc, CHUNK, H_B, DH_B).transpose(1, 0, 2, 3, 4)

    def step(args):
        qchunk, c = args
        start = c * CHUNK
        kb = lax.dynamic_slice_in_dim(kpad, start, BAND_LEN, axis=1)
        vb = lax.dynamic_slice_in_dim(vpad, start, BAND_LEN, axis=1)
        qpos = start + jnp.arange(CHUNK)
        kpos = start - BAND_PAST + jnp.arange(BAND_LEN)
        return band_attend(qchunk, kb, vb, qpos, kpos, rel_bias)

    o = lax.map(step, (qc, jnp.arange(nc)))
    return o.transpose(1, 0, 2, 3, 4).reshape(B, S, H_B, DH_B)


def memory_kv(mem, g, w_mem_kv):
    B = mem.shape[0]
    m = rms_norm(mem, g) @ w_mem_kv
    mk, mv = jnp.split(m, 2, axis=-1)
    return mk.reshape(B, N_MEM, H_C, DH_C), mv.reshape(B, N_MEM, H_C, DH_C)


def cross_attend(q, mk, mv):
    s = jnp.einsum("bqhd,bkhd->bhqk", q, mk).astype(jnp.float32) * (DH_C ** -0.5)
    p = jax.nn.softmax(s, axis=-1)
    return jnp.einsum("bhqk,bkhd->bqhd", p.astype(mv.dtype), mv)


def swiglu(h, w_ffn_in, w_ffn_out):
    a, b = jnp.split(h @ w_ffn_in, 2, axis=-1)
    return (jax.nn.silu(a) * b) @ w_ffn_out


def layer_forward(x, mk, mv, attend_a, attend_b, lam_init, norm_mix_pre, norm_mix_post, w_in, b_gate,
                  subln_a, w_br_a, w_br_b, w_br_c, w_out, norm_ffn_pre, norm_ffn_post, w_ffn_in, w_ffn_out):
    B, T, _ = x.shape
    h = rms_norm(x, norm_mix_pre)
    q_a, k_a, v_a, q_b, k_b, v_b, q_c, g = jnp.split(h @ w_in, SPLIT_IDX, axis=-1)
    q_a = q_a.reshape(B, T, H_A, DK_A)
    k_a = k_a.reshape(B, T, H_A, DK_A)
    v_a = v_a.reshape(B, T, H_A, DV_A)
    q_b = q_b.reshape(B, T, H_B, DH_B)
    k_b = k_b.reshape(B, T, H_B, DH_B)
    v_b = v_b.reshape(B, T, H_B, DH_B)
    q_c = q_c.reshape(B, T, H_C, DH_C)
    o_a = head_norm(attend_a(q_a, k_a, v_a), subln_a, lam_init).reshape(B, T, W_A)
    o_b = attend_b(q_b, k_b, v_b).reshape(B, T, W_B)
    o_c = cross_attend(q_c, mk, mv).reshape(B, T, W_C)
    gates = jax.nn.sigmoid(g + b_gate).reshape(B, T, N_BRANCH, D_MODEL)
    merged = (gates[..., 0, :] * (o_a @ w_br_a) + gates[..., 1, :] * (o_b @ w_br_b)
              + gates[..., 2, :] * (o_c @ w_br_c))
    x = x + rms_norm(merged @ w_out, norm_mix_post)
    x = x + rms_norm(swiglu(rms_norm(x, norm_ffn_pre), w_ffn_in, w_ffn_out), norm_ffn_post)
    return x, k_a, v_a, k_b, v_b


def setup_inputs(seed: int = 0) -> dict:
    key = jax.random.key(seed)
    ks = iter(jax.random.split(key, 48))
    nrm = lambda shape, scale: scale * jax.random.normal(next(ks), shape, jnp.float32)
    gain = lambda n: 1.0 + nrm((DEPTH, n), 0.05)
    lb = min(BAND_PAST, PAST_LEN)
    return {
        "x_prompt": nrm((BATCH, SEQ, D_MODEL), 1.0),
        "x_sample": nrm((DEC_BATCH, DEC_SEQ, D_MODEL), 1.0),
        "cache_a_k": nrm((DEPTH, DEC_BATCH, PAST_LEN, H_A, DK_A), 1.0),
        "cache_a_v": nrm((DEPTH, DEC_BATCH, PAST_LEN, H_A, DV_A), 1.0),
        "cache_b_k": nrm((DEPTH, DEC_BATCH, lb, H_B, DH_B), 1.0),
        "cache_b_v": nrm((DEPTH, DEC_BATCH, lb, H_B, DH_B), 1.0),
        "cache_mem_k": nrm((DEPTH, DEC_BATCH, N_MEM, H_C, DH_C), 1.0),
        "cache_mem_v": nrm((DEPTH, DEC_BATCH, N_MEM, H_C, DH_C), 1.0),
        "mem_prompt": nrm((BATCH, N_MEM, D_MODEL), 1.0),
        "norm_mix_pre": gain(D_MODEL),
        "norm_mix_post": gain(D_MODEL),
        "norm_mem": gain(D_MODEL),
        "w_in": nrm((DEPTH, D_MODEL, N_IN), D_MODEL ** -0.5),
        "b_gate": nrm((DEPTH, N_BRANCH * D_MODEL), 0.1),
        "lambda_q1": nrm((DEPTH, DH_A), 0.1),
        "lambda_k1": nrm((DEPTH, DH_A), 0.1),
        "lambda_q2": nrm((DEPTH, DH_A), 0.1),
        "lambda_k2": nrm((DEPTH, DH_A), 0.1),
        "subln_a": gain(DV_A),
        "rel_bias_b": nrm((DEPTH, H_B, 2 * MAX_REL + 1), 0.1),
        "w_mem_kv": nrm((DEPTH, D_MODEL, 2 * W_C), D_MODEL ** -0.5),
        "w_br_a": nrm((DEPTH, W_A, D_MODEL), W_A ** -0.5),
        "w_br_b": nrm((DEPTH, W_B, D_MODEL), W_B ** -0.5),
        "w_br_c": nrm((DEPTH, W_C, D_MODEL), W_C ** -0.5),
        "w_out": nrm((DEPTH, D_MODEL, D_MODEL), D_MODEL ** -0.5),
        "norm_ffn_pre": gain(D_MODEL),
        "norm_ffn_post": gain(D_MODEL),
        "w_ffn_in": nrm((DEPTH, D_MODEL, 2 * D_FF), D_MODEL ** -0.5),
        "w_ffn_out": nrm((DEPTH, D_FF, D_MODEL), D_FF ** -0.5),
    }


def reference(x_prompt, x_sample, cache_a_k, cache_a_v, cache_b_k, cache_b_v, cache_mem_k, cache_mem_v,
              mem_prompt, norm_mix_pre, norm_mix_post, norm_mem, w_in, b_gate, lambda_q1, lambda_k1,
              lambda_q2, lambda_k2, subln_a, rel_bias_b, w_mem_kv, w_br_a, w_br_b, w_br_c, w_out,
              norm_ffn_pre, norm_ffn_post, w_ffn_in, w_ffn_out):
    S = x_prompt.shape[1]
    T = x_sample.shape[1]
    P = cache_a_k.shape[2]
    Lb = cache_b_k.shape[2]
    Lb_prompt = min(BAND_PAST, S)
    yp, ys = x_prompt, x_sample
    akp, avp, bkp, bvp, mkp, mvp, aks, avs, bks, bvs = ([] for _ in range(10))
    for l in range(DEPTH):
        lam_init = 0.8 - 0.6 * math.exp(-0.3 * l)
        lam = diff_lambda(lambda_q1[l], lambda_k1[l], lambda_q2[l], lambda_k2[l], lam_init)
        shared = (lam_init, norm_mix_pre[l], norm_mix_post[l], w_in[l], b_gate[l], subln_a[l], w_br_a[l],
                  w_br_b[l], w_br_c[l], w_out[l], norm_ffn_pre[l], norm_ffn_post[l], w_ffn_in[l], w_ffn_out[l])
        rb = rel_bias_b[l]

        mk_p, mv_p = memory_kv(mem_prompt, norm_mem[l], w_mem_kv[l])
        attend_a_p = lambda q, k, v, lam=lam: diff_attention_prompt(q, k, v, lam)
        attend_b_p = lambda q, k, v, rb=rb: band_attention_prompt(q, k, v, rb)
        yp, ka, va, kb, vb = layer_forward(yp, mk_p, mv_p, attend_a_p, attend_b_p, *shared)
        akp.append(ka)
        avp.append(va)
        bkp.append(kb[:, S - Lb_prompt:])
        bvp.append(vb[:, S - Lb_prompt:])
        mkp.append(mk_p)
        mvp.append(mv_p)

        ca_k, ca_v, cb_k, cb_v = cache_a_k[l], cache_a_v[l], cache_b_k[l], cache_b_v[l]
        qpos = P + jnp.arange(T)

        def attend_a_s(q, k, v, lam=lam, ca_k=ca_k, ca_v=ca_v, qpos=qpos):
            k_all = jnp.concatenate([ca_k, k], axis=1)
            v_all = jnp.concatenate([ca_v, v], axis=1)
            return diff_attend(q, k_all, v_all, qpos, jnp.arange(P + T), lam)

        def attend_b_s(q, k, v, rb=rb, cb_k=cb_k, cb_v=cb_v, qpos=qpos):
            k_all = jnp.concatenate([cb_k, k], axis=1)
            v_all = jnp.concatenate([cb_v, v], axis=1)
            return band_attend(q, k_all, v_all, qpos, P - Lb + jnp.arange(Lb + T), rb)

        ys, ka, va, kb, vb = layer_forward(ys, cache_mem_k[l], cache_mem_v[l], attend_a_s, attend_b_s, *shared)
        aks.append(ka)
        avs.append(va)
        bks.append(kb)
        bvs.append(vb)

    return (yp, ys, jnp.stack(akp), jnp.stack(avp), jnp.stack(bkp), jnp.stack(bvp), jnp.stack(mkp),
            jnp.stack(mvp), jnp.stack(aks), jnp.stack(avs), jnp.stack(bks), jnp.stack(bvs))
```

```python
import math
import contextlib
import numpy as np
import concourse.bass as bass
import concourse.mybir as mybir
from concourse.bass_utils import run_bass_kernel_spmd

F32 = mybir.dt.float32
BF16 = mybir.dt.bfloat16
AF = mybir.ActivationFunctionType
ALU = mybir.AluOpType

NCORES = 8
D = 2048
SEQ = 2048
T = 512
PAST = 1024
NS = 16
DFF = 5632
NHC = 44
POSMAX = 2176
EPS = 1e-6
BIG = 30000.0
LAM_INIT = 0.2
THIRDS = (16, 16, 12)
OFF_QA, OFF_KA, OFF_VA, OFF_QB, OFF_KB, OFF_VB, OFF_QC, OFF_G = 0, 1024, 2048, 3072, 4096, 5120, 6144, 7168


class Op:
    __slots__ = ("eng", "fn", "deps", "signal", "sigval", "dma", "dsem", "dval")

    def __init__(self, eng, fn, dma):
        self.eng = eng
        self.fn = fn
        self.deps = []
        self.signal = False
        self.sigval = None
        self.dma = dma
        self.dsem = None
        self.dval = None


class Sched:
    ENGS = ("pe", "act", "dve", "pool", "sp")
    NDS = {"sp": 28, "pool": 24, "act": 16}
    RING_LIMIT = 800

    def __init__(self, nc):
        self.nc = nc
        self.q = {e: [] for e in self.ENGS}
        self.W = {}
        self.R = {}
        self.ds_uses = {k: [0] * n for k, n in self.NDS.items()}
        self.ds_last = {k: [None] * n for k, n in self.NDS.items()}
        self.ds_rr = {k: 0 for k in self.NDS}
        self.all_dma = []
        self.ring = []
        self.ring_sum = 0

    def op(self, eng, fn, reads=(), writes=(), dma=False, ndesc=0):
        o = Op(eng, fn, dma)
        deps = []
        for k in reads:
            w = self.W.get(k)
            if w is not None:
                deps.append(w)
        for k in writes:
            w = self.W.get(k)
            if w is not None:
                deps.append(w)
            r = self.R.get(k)
            if r:
                deps.extend(r[0].values())
                deps.extend(r[1])
        if dma and eng == "pool":
            while self.ring and self.ring_sum + ndesc > self.RING_LIMIT:
                old, nd = self.ring.pop(0)
                self.ring_sum -= nd
                deps.append(old)
            self.ring.append((o, ndesc))
            self.ring_sum += ndesc
        if dma:
            i = self.ds_rr[eng]
            self.ds_rr[eng] = (i + 1) % self.NDS[eng]
            self.ds_uses[eng][i] += 1
            o.dsem = (eng, i)
            o.dval = 16 * self.ds_uses[eng][i]
            if self.ds_last[eng][i] is not None:
                deps.append(self.ds_last[eng][i])
            self.ds_last[eng][i] = o
            self.all_dma.append(o)
        seen = set()
        for d in deps:
            if d is o or id(d) in seen:
                continue
            seen.add(id(d))
            if (not d.dma) and (not dma) and d.eng == eng and eng == "pe":
                continue
            d.signal = True
            o.deps.append(d)
        for k in writes:
            self.W[k] = o
            self.R[k] = ({}, [])
        for k in reads:
            r = self.R.get(k)
            if r is None:
                r = ({}, [])
                self.R[k] = r
            if dma:
                r[1].append(o)
            else:
                r[0][eng] = o
        self.q[eng].append(o)
        return o

    def barrier(self):
        lasts = []
        for e in self.ENGS:
            for o in reversed(self.q[e]):
                if not o.dma and o.fn is not None:
                    lasts.append(o)
                    break
        dm = list(self.all_dma)
        self.all_dma = []
        for e in ("pe", "act", "dve", "pool"):
            o = Op(e, None, False)
            for d in lasts:
                if d.eng == e:
                    continue
                d.signal = True
                o.deps.append(d)
            o.deps.extend(dm)
            self.q[e].append(o)

    def emit(self):
        nc = self.nc
        with contextlib.ExitStack() as st:
            esem = {e: st.enter_context(nc.semaphore("s_" + e)) for e in ("pe", "act", "dve", "pool")}
            dsem = {k: [st.enter_context(nc.semaphore("d%s%d" % (k, i))) for i in range(n)]
                    for k, n in self.NDS.items()}
            for e in self.ENGS:
                c = 0
                for o in self.q[e]:
                    if o.dma or o.fn is None:
                        continue
                    if o.signal:
                        c += 1
                        o.sigval = c
            block = st.enter_context(nc.Block())

            def run(eh, ename):
                seen = {}
                for o in self.q[ename]:
                    need = {}
                    for d in o.deps:
                        if d.dma:
                            key = d.dsem
                            val = d.dval
                        else:
                            key = d.eng
                            val = d.sigval
                        if need.get(key, 0) < val:
                            need[key] = val
                    for key, val in need.items():
                        if seen.get(key, 0) >= val:
                            continue
                        seen[key] = val
                        sem = dsem[key[0]][key[1]] if isinstance(key, tuple) else esem[key]
                        eh.wait_ge(sem, val)
                    if o.fn is None:
                        continue
                    inst = o.fn(eh)
                    if o.dma:
                        inst.then_inc(dsem[o.dsem[0]][o.dsem[1]], 16)
                    elif o.signal:
                        inst.then_inc(esem[ename], 1)
                if ename == "sp":
                    for k, n in self.NDS.items():
                        for i in range(n):
                            v = 16 * self.ds_uses[k][i]
                            if v > 0 and seen.get((k, i), 0) < v:
                                eh.wait_ge(dsem[k][i], v)

            @block.tensor
            def _(e):
                run(e, "pe")

            @block.scalar
            def _(e):
                run(e, "act")

            @block.vector
            def _(e):
                run(e, "dve")

            @block.gpsimd
            def _(e):
                run(e, "pool")

            @block.sync
            def _(e):
                run(e, "sp")


def host_consts():
    slopes = np.array([2.0 ** (-(h + 1)) for h in range(8)], np.float32)
    pos = np.arange(POSMAX)
    kaug = np.stack([np.ones(POSMAX), np.ones(POSMAX), 64.0 * (pos // 64), (pos % 64).astype(np.float64)]).astype(np.float32)
    qbase = np.stack([-64.0 * (pos // 64), -(pos % 64).astype(np.float64), np.ones(POSMAX), np.ones(POSMAX)])
    qaug = (slopes[:, None, None] * qbase[None]).astype(np.float32)
    k = np.arange(128)[:, None]
    q = np.arange(128)[None, :]
    same = (k // 64) == (q // 64)
    M = np.where(same & (k > q), -2.0 * (k - q), 0.0)
    inv = (k // 64) > (q // 64)
    maskA = np.stack([np.where(inv, -BIG, M * s) for s in slopes]).astype(np.float32)
    maskA = np.ascontiguousarray(maskA.transpose(1, 0, 2))
    maskB0 = np.where(inv, -BIG, 0.0).astype(np.float32)
    maskB4 = np.where((k < 64) & (q >= 64), -BIG, 0.0).astype(np.float32)
    ident = np.eye(128, dtype=np.float32)
    J = np.ascontiguousarray(ident[::-1])
    return dict(c_kaug=kaug, c_qaug=qaug, c_maskA=maskA, c_maskB0=maskB0, c_maskB4=maskB4, c_ident=ident, c_J=J)


DBG = {'stage': 0, 'blocks': None, 'skip': ()}


def build():
    nc = bass.Bass("TRN2", target_bir_lowering=False)
    S = Sched(nc)

    def din(name, shape, dt=F32):
        return nc.dram_tensor(name, list(shape), dt, kind="ExternalInput").ap()

    def dout(name, shape):
        return nc.dram_tensor(name, list(shape), F32, kind="ExternalOutput").ap()

    def dint(name, shape, dt):
        return nc.dram_tensor(name, list(shape), dt, kind="Internal").ap()

    xp = din("xp", [2, SEQ, D])
    xs = din("xs", [NS, D])
    cak = din("cak", [PAST, 1024])
    cav = din("cav", [PAST, 1024])
    cbk = din("cbk", [512, 1024])
    cbv = din("cbv", [512, 1024])
    cmk = din("cmk", [256, 1024])
    cmv = din("cmv", [256, 1024])
    memp = din("memp", [2, 256, D])
    w_in = din("w_in", [D, 13312])
    w_mem = din("w_mem", [D, D])
    w_br = [din("w_br%d" % i, [1024, D]) for i in range(3)]
    w_out = din("w_out", [D, D])
    w_fi = din("w_fi", [D, 2 * DFF])
    w_fo = din("w_fo", [DFF, D])
    g_mpre = din("g_mpre", [16, 128])
    g_mem = din("g_mem", [16, 128])
    g_fpre = din("g_fpre", [16, 128])
    g_mpost = din("g_mpost", [1, D])
    g_fpost = din("g_fpost", [1, D])
    b_gate = din("b_gate", [48, 128])
    lam_in = [din("lam%d" % i, [1, 64]) for i in range(4)]
    subln = din("subln", [1, 128])
    rbp = nc.dram_tensor("rbp", [8, 385], F32, kind="ExternalInput")
    c_kaug = din("c_kaug", [4, POSMAX])
    c_qaug = din("c_qaug", [8, 4, POSMAX])
    c_maskA = din("c_maskA", [128, 8, 128])
    c_maskB0 = din("c_maskB0", [128, 128])
    c_maskB4 = din("c_maskB4", [128, 128])
    c_ident = din("c_ident", [128, 128])
    c_J = din("c_J", [128, 128])

    yp = dout("yp", [2, SEQ, D])
    ys = dout("ys", [NS, D])
    akp = dout("akp", [2, SEQ, 1024])
    avp = dout("avp", [2, SEQ, 1024])
    bkp = dout("bkp", [2, 512, 1024])
    bvp = dout("bvp", [2, 512, 1024])
    mkp = dout("mkp", [2, 256, 1024])
    mvp = dout("mvp", [2, 256, 1024])
    aks = dout("aks", [NS, 1024])
    avs = dout("avs", [NS, 1024])
    bks = dout("bks", [NS, 1024])
    bvs = dout("bvs", [NS, 1024])

    kscrA = dint("kscrA", [3, 8, POSMAX, 128], BF16)
    vscrA = dint("vscrA", [3, 8, POSMAX, 130], BF16)
    kscrB = dint("kscrB", [3, 8, POSMAX, 128], BF16)
    vscrB = dint("vscrB", [3, 8, POSMAX, 130], BF16)
    mkscr = dint("mkscr", [3, 4, 256, 256], BF16)
    mvscr = dint("mvscr", [3, 4, 256, 258], BF16)
    x1s = dint("x1s", [4, 128, D], F32)
    WKV = dint("WKV", [16, 128, 4096], BF16)
    WQA = dint("WQA", [8, 128, 2048], BF16)
    WQB = dint("WQB", [8, 128, 2048], BF16)
    WQC = dint("WQC", [4, 128, 4096], BF16)
    WMg = dint("WMg", [8, 3, 128, 4096], BF16)
    WMb = dint("WMb", [8, 3, 128, 2048], BF16)
    WO = dint("WO", [8, 128, 4096], BF16)
    WFIa = dint("WFIa", [22, 128, 4096], BF16)
    WFIb = dint("WFIb", [22, 128, 4096], BF16)
    WFO = dint("WFO", [3, 8, 128, 4096], BF16)
    WMEM = dint("WMEM", [8, 128, 4096], BF16)

    st = contextlib.ExitStack()
    with st:
        def sb(name, shape, dt):
            return st.enter_context(nc.sbuf_tensor(name, list(shape), dt))

        ps = st.enter_context(nc.psum_tensor("ps", [128, 8, 512], F32))

        def psb(bank, lo, hi):
            return ps[:, bank, lo // 2:hi // 2].bitcast(BF16)

        def pk(bank, lo=0, hi=512):
            return [("ps", bank)]

        wbuf = [sb("wbuf%d" % i, [128, 4096], BF16) for i in range(4)]
        xt = [sb("xt%d" % i, [128, D], F32) for i in range(2)]
        xnb = sb("xnb", [128, D], BF16)
        mT = sb("mT", [128, 16, T], BF16)
        ident = sb("ident", [128, 128], BF16)
        identf = sb("identf", [128, 128], F32)
        Jf = sb("Jf", [128, 128], F32)
        maskA = sb("maskA", [128, 8, 128], BF16)
        maskB0 = sb("maskB0", [128, 128], BF16)
        maskB4 = sb("maskB4", [128, 128], BF16)
        TB = sb("TB", [128, 8, 2, 128], BF16)
        gsub = sb("gsub", [128, 128], F32)
        cols = sb("cols", [128, 128], F32)
        K1a = [sb("K1a%d" % g, [128, 512], BF16) for g in range(4)]
        K2a = [sb("K2a%d" % g, [128, 512], BF16) for g in range(4)]
        arena = sb("arena", [128, 44 * 1024], BF16)

        C_GPRE, C_GMEM, C_GFPRE, C_BG = 0, 16, 32, 48
        C_B0, C_LAM, C_NLAM = 96, 104, 105
        C_TMP = 106

        class Carver:
            def __init__(self):
                self.off = 0

            def get(self, nelem, dt, shape=None):
                nb = nelem * (4 if dt == F32 else 2)
                nb = (nb + 63) // 64 * 64
                a = self.off // 2
                self.off += nb
                assert self.off <= 44 * 1024 * 2, "arena overflow %d" % self.off
                v = arena[:, a:a + (nelem * 2 if dt == F32 else nelem)]
                if dt == F32:
                    v = v.bitcast(F32)
                return v

        ca = Carver()
        hT_f = ca.get(16 * T, BF16)
        hT = hT_f.rearrange("p (c t) -> p c t", t=T)
        oT_f = ca.get(24 * T, BF16)
        oT = oT_f.rearrange("p (c t) -> p c t", t=T)
        kraw = [ca.get(512, BF16).rearrange("p (t d) -> p t d", d=128) for _ in range(2)]
        vaug = [ca.get(4 * 130, BF16).rearrange("p (t d) -> p t d", d=130) for _ in range(2)]
        Q1 = [ca.get(512, BF16) for _ in range(2)]
        Q2 = [ca.get(512, BF16) for _ in range(2)]
        QB = [ca.get(512, BF16) for _ in range(2)]
        QC = ca.get(1024, BF16).rearrange("p (c t) -> p c t", t=T)
        KBt = [ca.get(512, BF16) for _ in range(4)]
        PT = [ca.get(512, BF16) for _ in range(4)]
        PTB = [ca.get(640, BF16) for _ in range(2)]
        krawC = ca.get(512, BF16).rearrange("p (t d) -> p t d", d=256)
        vaugC = ca.get(2 * 258, BF16).rearrange("p (t d) -> p t d", d=258)
        MKT = ca.get(512, BF16).rearrange("p (c m) -> p c m", m=256)
        sig = [ca.get(T, F32) for _ in range(2)]
        tmpm = ca.get(T, F32)
        macc = ca.get(T, F32)
        stg32 = [ca.get(256, F32) for _ in range(3)]
        stg16 = [ca.get(256, BF16) for _ in range(2)]
        stgv = [ca.get(2 * 130, BF16).rearrange("p (h d) -> p h d", d=130) for _ in range(2)]
        stgm = [ca.get(258, BF16) for _ in range(2)]
        t2 = [ca.get(128, F32) for _ in range(2)]
        Of = [ca.get(128, F32) for _ in range(2)]
        sqj = ca.get(256, F32)
        Ob = [ca.get(256, BF16) for _ in range(2)]
        Hk = [ca.get(128, F32) for _ in range(2)]
        lamt = [ca.get(64, F32) for _ in range(5)]
        att_end = ca.off
        cf = Carver()
        y1t = [cf.get(D, F32) for _ in range(4)]
        uT = cf.get(16 * T, BF16).rearrange("p (c t) -> p c t", t=T)
        gbc = cf.get(D, F32)
        sa = [cf.get(T, F32) for _ in range(2)]
        sqf = cf.get(D, BF16)

        def mm(out, lhsT, rhs, start, stop, r, w):
            S.op("pe", lambda e: e.matmul(out, lhsT, rhs, start=start, stop=stop, skip_group_check=True), r, w)

        def tp(out, in_, idn, r, w):
            S.op("pe", lambda e: e.transpose(out, in_, idn), list(r) + ["ident"], w)

        def act(out, in_, func, r, w, bias=None, scale=None, accum=None):
            kw = {}
            if bias is not None:
                kw["bias"] = bias
            if scale is not None:
                kw["scale"] = scale
            if accum is not None:
                kw["accum_out"] = accum
            S.op("act", lambda e: e.activation(out=out, in_=in_, func=func, **kw), r, w)

        def ndesc_of(ap):
            dims = [list(x) for x in ap.ap]
            tot = 1
            for st_, n_ in dims:
                tot *= n_
            run = 1
            exp_ = 1
            for st_, n_ in reversed(dims):
                if n_ == 1:
                    continue
                if st_ == exp_:
                    run *= n_
                    exp_ = st_ * n_
                else:
                    break
            return max(1, tot // run)

        def dma(q, out, in_, r, w, slow=False):
            nd = 0
            if q == "pool":
                nd = (2 * max(ndesc_of(out), ndesc_of(in_)) + 15) // 16 + 4
            if slow:
                return S.op(q, lambda e: e.dma_start(out=out, in_=in_, allow_slow_non_contiguous=True), r, w, dma=True,
                            ndesc=nd)
            return S.op(q, lambda e: e.dma_start(out=out, in_=in_), r, w, dma=True, ndesc=nd)

        def vts(out, in0, s1, s2, op0, op1, r, w, eng="dve"):
            if op1 is None:
                S.op(eng, lambda e: e.tensor_scalar(out, in0, s1, None, op0), r, w)
            else:
                S.op(eng, lambda e: e.tensor_scalar(out, in0, s1, s2, op0, op1), r, w)

        def vtt(out, in0, in1, op, r, w, eng="dve"):
            S.op(eng, lambda e: e.tensor_tensor(out, in0, in1, op), r, w)

        def vstt(out, in0, scalar, in1, op0, op1, r, w):
            S.op("dve", lambda e: e.scalar_tensor_tensor(out, in0, scalar, in1, op0, op1), r, w)

        def vcopy(out, in_, r, w, eng="dve"):
            S.op(eng, lambda e: e.tensor_copy(out, in_), r, w)

        col_ctr = [0]

        def tcol():
            c = C_TMP + (col_ctr[0] % 20)
            col_ctr[0] += 1
            return c

        def rstd_from_ss(ss_c, n, r_c, rows, extra_bias=0.0):
            l_c = tcol()
            act(cols[0:rows, l_c:l_c + 1], cols[0:rows, ss_c:ss_c + 1], AF.Ln, [("col", ss_c)], [("col", l_c)],
                bias=EPS, scale=1.0 / n)
            act(cols[0:rows, r_c:r_c + 1], cols[0:rows, l_c:l_c + 1], AF.Exp, [("col", l_c)], [("col", r_c)],
                bias=extra_bias, scale=-0.5)

        wctr = [0]

        def wload(src_ap, nelem, *keys):
            i = wctr[0] % 4
            wctr[0] += 1
            if keys[0] not in wdone:
                for key in keys:
                    for (dst3, src3) in wreg[key]:
                        e0 = dst3.offset - src_ap.offset
                        nk, ncol = dst3.shape[1], dst3.shape[2]
                        dma("pool", wbuf[i][:, e0:e0 + nk * ncol].rearrange("p (k j) -> p k j", j=ncol), src3, [],
                            [("w", i)])
                dma("sp", src_ap, wbuf[i][:, 0:nelem], [("w", i)], list(keys))
                wdone.add(keys[0])
            else:
                dma("sp", wbuf[i][:, 0:nelem], src_ap, list(keys), [("w", i)])
            return wbuf[i], ("w", i)

        dma("pool", ident[:], c_ident, [], ["ident"])
        dma("sp", identf[:], c_ident, [], ["identf"])
        dma("sp", Jf[:], c_J, [], ["Jf"])
        dma("pool", maskA[:], c_maskA, [], ["maskA"])
        dma("pool", maskB0[:], c_maskB0, [], ["maskB0"])
        dma("pool", maskB4[:], c_maskB4, [], ["maskB4"])
        dma("sp", gsub[:], subln.partition_broadcast(128), [], ["gsub"])
        for g in range(4):
            dma("pool", K1a[g][64:68, :], c_kaug[:, g * 512:(g + 1) * 512], [], [("K1aug", g)])
            dma("pool", K2a[g][64:68, :], c_kaug[:, g * 512:(g + 1) * 512], [], [("K2aug", g)])
        for src, nrow, c0 in ((g_mpre, 16, C_GPRE), (g_mem, 16, C_GMEM), (g_fpre, 16, C_GFPRE), (b_gate, 48, C_BG)):
            dma("sp", xt[0][0:nrow, 0:128], src, [], ["xt0"])
            mm(ps[:, 0, 0:nrow], xt[0][0:nrow, 0:128], identf[0:nrow, 0:nrow], True, True, ["xt0", "identf"], pk(0, 0, nrow))
            vcopy(cols[:, c0:c0 + nrow], ps[:, 0, 0:nrow], pk(0, 0, nrow), [("colblk", c0)])
        for h in range(8):
            src = bass.AP(tensor=rbp, offset=h * 385 + 128, ap=[[0, 128], [1, 1]])
            dma("sp", cols[:, C_B0 + h:C_B0 + h + 1], src, [], [("b0", h)])
        for i in range(4):
            dma("sp", lamt[i], lam_in[i].partition_broadcast(128), [], [("lamt", i)])
        c1, c2 = tcol(), tcol()
        vtt(lamt[4], lamt[0], lamt[1], ALU.mult, [("lamt", 0), ("lamt", 1)], [("lamt", 4)])
        S.op("dve", lambda e: e.tensor_reduce(cols[:, c1:c1 + 1], lamt[4], mybir.AxisListType.X, ALU.add), [("lamt", 4)], [("col", c1)])
        vtt(lamt[4], lamt[2], lamt[3], ALU.mult, [("lamt", 2), ("lamt", 3)], [("lamt", 4)])
        S.op("dve", lambda e: e.tensor_reduce(cols[:, c2:c2 + 1], lamt[4], mybir.AxisListType.X, ALU.add), [("lamt", 4)], [("col", c2)])
        act(cols[:, c1:c1 + 1], cols[:, c1:c1 + 1], AF.Exp, [("col", c1)], [("col", c1)])
        act(cols[:, c2:c2 + 1], cols[:, c2:c2 + 1], AF.Exp, [("col", c2)], [("col", c2)])
        vtt(cols[:, C_LAM:C_LAM + 1], cols[:, c1:c1 + 1], cols[:, c2:c2 + 1], ALU.subtract, [("col", c1), ("col", c2)], ["lam"])
        vts(cols[:, C_NLAM:C_NLAM + 1], cols[:, C_LAM:C_LAM + 1], LAM_INIT, -1.0, ALU.add, ALU.mult, ["lam"], ["nlam"])
        for h in range(8):
            for d in range(2):
                hk = Hk[(h * 2 + d) % 2]
                hkey = ("Hk", (h * 2 + d) % 2)
                src = bass.AP(tensor=rbp, offset=h * 385 + 129 - 128 * d, ap=[[1, 128], [1, 128]])
                dma("sp", hk, src, [], [hkey])
                bk = 1 + (h * 2 + d) % 2
                mm(ps[:, bk, 0:128], hk, Jf[:], True, True, [hkey, "Jf"], pk(bk, 0, 128))
                vcopy(TB[:, h, d, :], ps[:, bk, 0:128], pk(bk, 0, 128), [("TB", h, d)])

        wreg = {}
        wdone = set()

        def conv(dst, src, key):
            wreg.setdefault(key, []).append((dst, src))

        def kview(wap, r0, nk, c0, ncol):
            return wap[r0:r0 + nk * 128, c0:c0 + ncol].rearrange("(k p) j -> p k j", p=128)

        def d3(dst, nk, ncol, e0=0):
            return dst[:, e0:e0 + nk * ncol].rearrange("p (k j) -> p k j", j=ncol)

        kvoffs = [OFF_KA, OFF_VA, OFF_KB, OFF_VB]
        for nb in range(8):
            conv(d3(WMEM[nb], 16, 256), kview(w_mem, 0, 16, nb * 256, 256), ("WMEM", nb))
        for nb in range(16):
            c0 = kvoffs[nb // 4] + (nb % 4) * 256
            conv(d3(WKV[nb], 16, 256), kview(w_in, 0, 16, c0, 256), ("WKV", nb))
        for h in range(8):
            conv(d3(WQA[h], 16, 128), kview(w_in, 0, 16, OFF_QA + h * 128, 128), ("WQA", h))
        for h in range(8):
            conv(d3(WQB[h], 16, 128), kview(w_in, 0, 16, OFF_QB + h * 128, 128), ("WQB", h))
        for h in range(4):
            conv(d3(WQC[h], 16, 256), kview(w_in, 0, 16, OFF_QC + h * 256, 256), ("WQC", h))
        for np_ in range(8):
            for br in range(3):
                conv(d3(WMg[np_, br], 16, 256), kview(w_in, 0, 16, OFF_G + br * 2048 + np_ * 256, 256), ("WMg", np_, br))
                conv(d3(WMb[np_, br], 8, 256), kview(w_br[br], 0, 8, np_ * 256, 256), ("WMb", np_, br))
        for nb in range(8):
            conv(d3(WO[nb], 16, 256), kview(w_out, 0, 16, nb * 256, 256), ("WO", nb))
        for pr in range(22):
            conv(d3(WFIa[pr], 16, 256), kview(w_fi, 0, 16, pr * 256, 256), ("WFIa", pr))
            conv(d3(WFIb[pr], 16, 256), kview(w_fi, 0, 16, DFF + pr * 256, 256), ("WFIb", pr))
        k0_ = 0
        for t3 in range(3):
            nk = THIRDS[t3]
            for nb in range(8):
                conv(d3(WFO[t3, nb], nk, 256), kview(w_fo, k0_ * 128, nk, nb * 256, 256), ("WFO", t3, nb))
            k0_ += nk

        ones16 = sb("ones16", [128, 8, 2], BF16)
        S.op("pool", lambda e: e.memset(ones16[:], 1.0), [], ["ones16"])
        S.op("pool", lambda e: e.memset(xnb[:], 0.0), [], ["xnb"])
        for kind_, (scr, wd) in enumerate(((kscrA, 128), (vscrA, 130), (kscrB, 128), (vscrB, 130))):
            dma("pool", scr[2, :, 1024:1152, :].rearrange("h p d -> p h d"),
                xnb[:, 0:8 * wd].rearrange("p (h d) -> p h d", d=wd), ["xnb"], [("scrpad", kind_)])
        deferred = []

        def defer(*args, **kw):
            deferred.append(lambda: dma(*args, **kw))

        for kt in range(8):
            defer("pool", kscrA[2, :, kt * 128:(kt + 1) * 128, :].rearrange("h p d -> p h d"),
                  cak[kt * 128:(kt + 1) * 128, :].rearrange("p (h d) -> p h d", d=128), [], [("kvA", "k", 2, kt)])
            defer("pool", vscrA[2, :, kt * 128:(kt + 1) * 128, 0:128].rearrange("h p d -> p h d"),
                  cav[kt * 128:(kt + 1) * 128, :].rearrange("p (h d) -> p h d", d=128), [], [("kvA", "v", 2, kt)])
            defer("pool", vscrA[2, :, kt * 128:(kt + 1) * 128, 128:130].rearrange("h p d -> p h d"),
                  ones16[:, :, :], ["ones16"], [("kvA", "v1", 2, kt)], slow=True)
        for kt in range(4, 8):
            r0 = (kt - 4) * 128
            defer("pool", kscrB[2, :, kt * 128:(kt + 1) * 128, :].rearrange("h p d -> p h d"),
                  cbk[r0:r0 + 128, :].rearrange("p (h d) -> p h d", d=128), [], [("kvB", "k", 2, kt)])
            defer("pool", vscrB[2, :, kt * 128:(kt + 1) * 128, 0:128].rearrange("h p d -> p h d"),
                  cbv[r0:r0 + 128, :].rearrange("p (h d) -> p h d", d=128), [], [("kvB", "v", 2, kt)])
            defer("pool", vscrB[2, :, kt * 128:(kt + 1) * 128, 128:130].rearrange("h p d -> p h d"),
                  ones16[:, :, :], ["ones16"], [("kvB", "v1", 2, kt)], slow=True)
        for mt in range(2):
            defer("pool", mkscr[2, :, mt * 128:(mt + 1) * 128, :].rearrange("h p d -> p h d"),
                  cmk[mt * 128:(mt + 1) * 128, :].rearrange("p (h d) -> p h d", d=256), [], [("kvM", "k", 2, mt)])
            defer("pool", mvscr[2, :, mt * 128:(mt + 1) * 128, 0:256].rearrange("h p d -> p h d"),
                  cmv[mt * 128:(mt + 1) * 128, :].rearrange("p (h d) -> p h d", d=256), [], [("kvM", "v", 2, mt)])
            defer("pool", mvscr[2, :, mt * 128:(mt + 1) * 128, 256:258].rearrange("h p d -> p h d"),
                  ones16[:, 0:4, :], ["ones16"], [("kvM", "v1", 2, mt)], slow=True)

        def flush_deferred(n):
            for _ in range(min(n, len(deferred))):
                deferred.pop(0)()

        for i in range(2):
            S.op("pool", lambda e, i=i: e.memset(stgv[i][:, :, 128:130], 1.0), [], [("stgv1", i)])
            S.op("pool", lambda e, i=i: e.memset(stgm[i][:, 256:258], 1.0), [], [("stgm1", i)])

        psrr = [0]

        def nextps(banks):
            b = banks[psrr[0] % len(banks)]
            psrr[0] += 1
            return b

        hrr = [0]

        def nexthalf():
            v = hrr[0] % 6
            hrr[0] += 1
            return v, 0

        xrr = [0]

        def norm_transpose(src_rows, rows, gcol0, dstT, dstkey, col0, srckey=None, from_tile=None):
            if from_tile is None:
                xi = xrr[0] % 2
                xrr[0] += 1
                xtile = xt[xi]
                xkey = "xt%d" % xi
                dma("sp", xtile[0:rows, :], src_rows, [srckey] if srckey else [], [xkey])
            else:
                xtile, xkey = from_tile
            ss_c, r_c = tcol(), tcol()
            act(xnb[0:rows, :], xtile[0:rows, :], AF.Square, [xkey], ["xnb", ("col", ss_c)],
                accum=cols[0:rows, ss_c:ss_c + 1])
            if 'nt1' in DBG['skip']:
                return
            rstd_from_ss(ss_c, D, r_c, rows)
            if 'nt2' in DBG['skip']:
                return
            vts(xnb[0:rows, :], xtile[0:rows, :], cols[0:rows, r_c:r_c + 1], None, ALU.mult, None,
                [xkey, ("col", r_c)], ["xnb"])
            if 'nt3' in DBG['skip']:
                return
            for c4 in range(4):
                half = c4 % 2
                for i in range(4):
                    c = c4 * 4 + i
                    tp(psb(6 + half, i * 128, i * 128 + rows), xnb[0:rows, c * 128:(c + 1) * 128],
                       ident[0:rows, 0:rows], ["xnb"], [("ps", 6 + half)])
                for i in range(4):
                    c = c4 * 4 + i
                    vts(dstT[:, c, col0:col0 + rows], psb(6 + half, i * 128, i * 128 + rows),
                        cols[:, gcol0 + c:gcol0 + c + 1], None, ALU.mult, None,
                        [("ps", 6 + half), ("colblk", gcol0)], [(dstkey, c)])

        def memory_kv(s):
            for tt in range(2):
                norm_transpose(memp[s, tt * 128:(tt + 1) * 128, :], 128, C_GMEM, hT, "hT", tt * 128)
            if 'memA' in DBG['skip']:
                return
            for nb in range(8):
                wt, wk = wload(WMEM[nb], 4096, ("WMEM", nb))
                isv = nb >= 4
                hh = (nb % 4)
                for tt in range(2):
                    b, half = nexthalf()
                    pkey = ("ps", b)
                    for k in range(16):
                        mm(ps[:, b, half:half + 256], hT[:, k, tt * 128:(tt + 1) * 128], wt[:, k * 256:(k + 1) * 256],
                           k == 0, k == 15, [wk, ("hT", k)], [pkey])
                    si = nextps([0, 1, 2])
                    act(stg32[si], ps[:, b, half:half + 256], AF.Identity, [pkey], [("stg32", si)])
                    dst = (mvp if isv else mkp)[s, tt * 128:(tt + 1) * 128, hh * 256:(hh + 1) * 256]
                    if 'memB' not in DBG['skip']:
                        dma("act", dst, stg32[si], [("stg32", si)], [])
                    if 'memC' in DBG['skip']:
                        continue
                    mi = nextps([0, 1])
                    vcopy(stgm[mi][:, 0:256], stg32[si], [("stg32", si)], [("stgm", mi)])
                    if 'memD' in DBG['skip']:
                        continue
                    if isv:
                        dma("act", mvscr[s, hh, tt * 128:(tt + 1) * 128, :], stgm[mi][:, 0:258],
                            [("stgm", mi), ("stgm1", mi)], [("kvM", "v", s, tt, hh)])
                    else:
                        dma("act", mkscr[s, hh, tt * 128:(tt + 1) * 128, :], stgm[mi][:, 0:256],
                            [("stgm", mi)], [("kvM", "k", s, tt, hh)])

        def memkeys(s, hc):
            if s == 2:
                return [("kvM", "k", 2, 0), ("kvM", "k", 2, 1)], [("kvM", "v", 2, 0), ("kvM", "v", 2, 1),
                                                                   ("kvM", "v1", 2, 0), ("kvM", "v1", 2, 1)]
            return [("kvM", "k", s, 0, hc), ("kvM", "k", s, 1, hc)], [("kvM", "v", s, 0, hc), ("kvM", "v", s, 1, hc)]

        def block(s, j, nq):
            sample = (s == 2)
            nqt = (nq + 127) // 128
            qw = min(128, nq)
            pos0 = j * T
            xsrc = xs if sample else xp[s, pos0:pos0 + T, :]
            kmin = 4 if sample else 0

            def xrows(tt):
                return xsrc[tt * 128:tt * 128 + qw, :]

            if sample:
                flush_deferred(1000)

            for tt in range(nqt):
                norm_transpose(xrows(tt), qw, C_GPRE, hT, "hT", tt * 128)

            if DBG['stage'] and DBG['stage'] <= 1:
                S.barrier()
                return
            for nb in range(16):
                wt, wk = wload(WKV[nb], 4096, ("WKV", nb))
                kind = nb // 4
                h0 = (nb % 4) * 2
                for tt in range(nqt):
                    b, half = nexthalf()
                    pso = ps[0:qw, b, half:half + 256]
                    pkey = ("ps", b)
                    for k in range(16):
                        mm(pso, hT[:, k, tt * 128:tt * 128 + qw], wt[:, k * 256:(k + 1) * 256],
                           k == 0, k == 15, [wk, ("hT", k)], [pkey])
                    r0 = pos0 + tt * 128
                    odst = None
                    if sample:
                        odst = (aks, avs, bks, bvs)[kind][0:qw, h0 * 128:h0 * 128 + 256]
                    elif kind < 2:
                        odst = (akp, avp)[kind][s, r0:r0 + 128, h0 * 128:h0 * 128 + 256]
                    elif j == 3:
                        odst = (bkp, bvp)[kind - 2][s, tt * 128:(tt + 1) * 128, h0 * 128:h0 * 128 + 256]
                    esrc, ekey = pso, pkey
                    if odst is not None:
                        si = nextps([0, 1, 2])
                        act(stg32[si][0:qw, :], pso, AF.Identity, [pkey], [("stg32", si)])
                        dma("act", odst, stg32[si][0:qw, :], [("stg32", si)], [])
                        esrc, ekey = stg32[si][0:qw, :], ("stg32", si)
                    kt = r0 // 128
                    if kind in (0, 2):
                        mi = nextps([0, 1])
                        vcopy(stg16[mi][0:qw, :], esrc, [ekey], [("stg16", mi)])
                        scr = kscrA if kind == 0 else kscrB
                        dma("act", scr[s, h0:h0 + 2, r0:r0 + qw, :].rearrange("h p d -> p h d"),
                            stg16[mi][0:qw, :].rearrange("p (h d) -> p h d", d=128),
                            [("stg16", mi)] + ([("scrpad", kind)] if sample else []), [("kv", kind, s, kt, h0)])
                    else:
                        mi = nextps([0, 1])
                        vcopy(stgv[mi][0:qw, :, 0:128], esrc.rearrange("p (h d) -> p h d", d=128), [ekey], [("stgv", mi)])
                        scr = vscrA if kind == 1 else vscrB
                        dma("act", scr[s, h0:h0 + 2, r0:r0 + qw, :].rearrange("h p d -> p h d"),
                            stgv[mi][0:qw, :, :],
                            [("stgv", mi), ("stgv1", mi)] + ([("scrpad", kind)] if sample else []),
                            [("kv", kind, s, kt, h0)])

            def kvkeys(branch, which, kts, h):
                out = []
                kind = (0 if branch == "A" else 2) + which
                for kt in kts:
                    if sample and kt < 8:
                        nm = "kvA" if branch == "A" else "kvB"
                        out.append((nm, "kv"[which], 2, kt))
                        if which == 1:
                            out.append((nm, "v1", 2, kt))
                    else:
                        out.append(("kv", kind, s, kt, (h // 2) * 2))
                        if sample:
                            out.append(("scrpad", kind))
                return out

            if DBG['stage'] and DBG['stage'] <= 2:
                S.barrier()
                return
            def run_pipeline(steps, depth=1, post_delay=1):
                n = len(steps)
                issued = 0
                pending = []
                for i in range(n):
                    while issued < n and issued <= i + depth:
                        if issued > i and steps[issued].get("nolook"):
                            break
                        for f in steps[issued]["pre"]:
                            f()
                        steps[issued]["qk"]()
                        issued += 1
                    post = steps[i]["pv"]()
                    pending = [(d - 1, f) for (d, f) in pending]
                    while pending and pending[0][0] <= 0:
                        pending.pop(0)[1]()
                    if post is not None:
                        pending.append((post_delay, post))
                for (_, f) in pending:
                    f()

            fbc_ = [0]

            def next_fb():
                v = fbc_[0] % 2
                fbc_[0] += 1
                return v

            gbc_ = [0]

            def next_gb():
                v = gbc_[0] % 2
                gbc_[0] += 1
                return v

            ksc_ = [0]

            def next_ks():
                v = ksc_[0] % 4
                ksc_[0] += 1
                return v

            TPB = 7
            QPB = 7
            SBA = (0, 1, 6)
            stepctr = [0]

            kt_max = 4 * j + nqt - 1
            stepsA = []
            for h in range(8):
                hb = h % 2

                def head_pre(h=h, hb=hb):
                    wt, wk = wload(WQA[h], 2048, ("WQA", h))
                    for k in range(16):
                        mm(ps[:, QPB, 0:nq], wt[:, k * 128:(k + 1) * 128], hT[:, k, 0:nq], k == 0, k == 15,
                           [wk, ("hT", k)], pk(QPB))
                    act(Q1[hb][0:64, 0:nq], ps[0:64, QPB, 0:nq], AF.Identity, pk(QPB), [("Q1", hb)], scale=0.125)
                    act(Q2[hb][0:64, 0:nq], ps[64:128, QPB, 0:nq], AF.Identity, pk(QPB), [("Q2", hb)], scale=0.125)
                    dma("pool", Q1[hb][64:68, 0:nq], c_qaug[h, :, pos0:pos0 + nq], [], [("Q1a", hb)])
                    dma("pool", Q2[hb][64:68, 0:nq], c_qaug[h, :, pos0:pos0 + nq], [], [("Q2a", hb)])

                first_of_head = True
                for g in range(kt_max // 4 + 1):
                    kts = list(range(4 * g, min(4 * g + 3, kt_max) + 1))
                    n_kt = len(kts)
                    gb = next_gb()

                    def group_pre(h=h, g=g, kts=kts, n_kt=n_kt, gb=gb):
                        dma("sp", kraw[gb][:, 0:n_kt, :],
                            kscrA[s, h, 4 * g * 128:(4 * g + n_kt) * 128, :].rearrange("(t p) d -> p t d", p=128),
                            kvkeys("A", 0, kts, h), [("kraw", gb)])
                        dma("sp", vaug[gb][:, 0:n_kt, :],
                            vscrA[s, h, 4 * g * 128:(4 * g + n_kt) * 128, :].rearrange("(t p) d -> p t d", p=128),
                            kvkeys("A", 1, kts, h), [("vaug", gb)])
                        for i in range(n_kt):
                            tp(psb(TPB, i * 128, (i + 1) * 128), kraw[gb][:, i, :], ident[:], [("kraw", gb)], pk(TPB))
                        act(K1a[g][0:64, 0:n_kt * 128], psb(TPB, 0, n_kt * 128)[0:64, :], AF.Identity, pk(TPB),
                            [("K1", g)])
                        act(K2a[g][0:64, 0:n_kt * 128], psb(TPB, 0, n_kt * 128)[64:128, :], AF.Identity, pk(TPB),
                            [("K2", g)])

                    first_of_group = True
                    for i, kt in enumerate(kts):
                        t = kt - 4 * j
                        diag = t >= 0
                        c0 = t * 128 if diag else 0
                        for m in range(2):
                            sc = stepctr[0]
                            stepctr[0] += 1
                            sbk = SBA[sc % 3]
                            pb = sc % 4
                            pre = []
                            if first_of_head:
                                pre.append(head_pre)
                                first_of_head = False
                            if first_of_group:
                                pre.append(group_pre)
                                first_of_group = False

                            def qk(h=h, hb=hb, g=g, i=i, m=m, diag=diag, c0=c0, sbk=sbk, pb=pb):
                                Km = (K1a, K2a)[m][g]
                                Qm = (Q1, Q2)[m][hb]
                                kkeys = [("K1", g), ("K1aug", g)] if m == 0 else [("K2", g), ("K2aug", g)]
                                qkeys = [("Q1", hb), ("Q1a", hb)] if m == 0 else [("Q2", hb), ("Q2a", hb)]
                                mm(ps[:, sbk, c0:nq], Km[0:68, i * 128:(i + 1) * 128], Qm[0:68, c0:nq], True, not diag,
                                   kkeys + qkeys, pk(sbk))
                                if diag:
                                    mm(ps[:, sbk, c0:c0 + qw], ident[:], maskA[:, h, 0:qw], False, True,
                                       ["ident", "maskA"], pk(sbk))
                                act(PT[pb][:, c0:nq], ps[:, sbk, c0:nq], AF.Exp, pk(sbk), [("PT", pb)])

                            def pv(h=h, gb=gb, i=i, kt=kt, m=m, t=t, diag=diag, pb=pb):
                                for qt in range(max(t, 0), nqt):
                                    bank = 2 + qt
                                    off = m * 256
                                    mm(ps[0:qw, bank, off:off + 129], PT[pb][:, qt * 128:qt * 128 + qw],
                                       vaug[gb][:, i, 0:129], (kt == 0) and m == 0, kt == 4 * j + qt,
                                       [("PT", pb), ("vaug", gb)], pk(bank))
                                if m == 1 and diag and t < nqt:
                                    qt = t
                                    o1 = ps[0:qw, 2 + qt, 0:129]
                                    o2 = ps[0:qw, 2 + qt, 256:385]
                                    kk_ = pk(2 + qt)
                                    fb = next_fb()
                                    r1, r2, ssc, rsc = tcol(), tcol(), tcol(), tcol()
                                    S.op("dve", lambda e: e.reciprocal(cols[0:qw, r1:r1 + 1], o1[:, 128:129]), kk_,
                                         [("col", r1)])
                                    S.op("dve", lambda e: e.reciprocal(cols[0:qw, r2:r2 + 1], o2[:, 128:129]), kk_,
                                         [("col", r2)])
                                    vtt(cols[0:qw, r2:r2 + 1], cols[0:qw, r2:r2 + 1], cols[0:qw, C_NLAM:C_NLAM + 1],
                                        ALU.mult, [("col", r2), "nlam"], [("col", r2)])
                                    vts(t2[fb][0:qw, :], o2[:, 0:128], cols[0:qw, r2:r2 + 1], None, ALU.mult, None,
                                        kk_ + [("col", r2)], [("t2", fb)])
                                    vstt(Of[fb][0:qw, :], o1[:, 0:128], cols[0:qw, r1:r1 + 1], t2[fb][0:qw, :], ALU.mult,
                                         ALU.add, kk_ + [("col", r1), ("t2", fb)], [("Of", fb)])
                                    act(sqj[0:qw, 0:128], Of[fb][0:qw, :], AF.Square, [("Of", fb)], ["sqj", ("col", ssc)],
                                        accum=cols[0:qw, ssc:ssc + 1])
                                    rstd_from_ss(ssc, 128, rsc, qw, extra_bias=math.log(1.0 - LAM_INIT))
                                    vstt(Ob[fb][0:qw, 0:128], Of[fb][0:qw, :], cols[0:qw, rsc:rsc + 1], gsub[0:qw, :],
                                         ALU.mult, ALU.mult, [("Of", fb), ("col", rsc), "gsub"], [("Ob", fb)])

                                    def post():
                                        tp(psb(TPB, 512, 512 + qw), Ob[fb][0:qw, 0:128], ident[0:qw, 0:qw], [("Ob", fb)],
                                           pk(TPB))
                                        vcopy(oT[:, h, qt * 128:qt * 128 + qw], psb(TPB, 512, 512 + qw), pk(TPB),
                                              [("oT", h)])
                                    return post
                                return None

                            stepsA.append(dict(pre=pre, qk=qk, pv=pv))
            run_pipeline(stepsA, depth=2, post_delay=3)

            if DBG['stage'] and DBG['stage'] <= 3:
                S.barrier()
                return
            stepsB = []
            for h in range(8):
                hb = h % 2
                slot_of = {}
                groups = []
                for g in (j - 1, j):
                    kts = [kt for kt in range(4 * g, 4 * g + 4) if kmin <= kt <= 4 * j + nqt - 1 and kt >= 0]
                    if not kts:
                        continue
                    gb = next_gb()
                    ks = next_ks()
                    groups.append((kts, gb, ks))
                    for i, kt in enumerate(kts):
                        slot_of[kt] = (ks, gb, i)

                def head_preB(h=h, hb=hb, groups=groups):
                    wt, wk = wload(WQB[h], 2048, ("WQB", h))
                    for k in range(16):
                        mm(ps[:, QPB, 0:nq], wt[:, k * 128:(k + 1) * 128], hT[:, k, 0:nq], k == 0, k == 15,
                           [wk, ("hT", k)], pk(QPB))
                    act(QB[hb][:, 0:nq], ps[:, QPB, 0:nq], AF.Identity, pk(QPB), [("QB", hb)], scale=128.0 ** -0.5)
                    for (kts, gb, ks) in groups:
                        n_kt = len(kts)
                        k0 = kts[0]
                        dma("sp", kraw[gb][:, 0:n_kt, :],
                            kscrB[s, h, k0 * 128:(k0 + n_kt) * 128, :].rearrange("(t p) d -> p t d", p=128),
                            kvkeys("B", 0, kts, h), [("kraw", gb)])
                        dma("sp", vaug[gb][:, 0:n_kt, :],
                            vscrB[s, h, k0 * 128:(k0 + n_kt) * 128, :].rearrange("(t p) d -> p t d", p=128),
                            kvkeys("B", 1, kts, h), [("vaug", gb)])
                        for i in range(n_kt):
                            tp(psb(TPB, i * 128, (i + 1) * 128), kraw[gb][:, i, :], ident[:], [("kraw", gb)], pk(TPB))
                        vcopy(KBt[ks][:, 0:n_kt * 128], psb(TPB, 0, n_kt * 128), pk(TPB), [("KBt", ks)])

                for qt in range(nqt):
                    gi = 4 * j + qt
                    ds = [d for d in range(5) if gi - d >= kmin and gi - d >= 0]
                    sc = stepctr[0]
                    stepctr[0] += 1
                    pair = sc % 2
                    pb = sc % 2
                    ob = 4 + (sc % 2)

                    def qkB(h=h, hb=hb, qt=qt, gi=gi, ds=ds, pair=pair, pb=pb, slot_of=slot_of):
                        bk0, bk1 = 2 * pair, 2 * pair + 1
                        started = {bk0: False, bk1: False}
                        for d in ds:
                            kt = gi - d
                            ks, gb, i = slot_of[kt]
                            bk = bk0 if d < 2 else bk1
                            off = (d if d < 2 else d - 2) * 128
                            extra = []
                            if d == 0:
                                extra = [(TB[:, h, 0, 0:qw], ("TB", h, 0)), (maskB0[:, 0:qw], "maskB0")]
                            elif d == 1:
                                extra = [(TB[:, h, 1, 0:qw], ("TB", h, 1))]
                            elif d == 4:
                                extra = [(maskB4[:, 0:qw], "maskB4")]
                            mm(ps[:, bk, off:off + qw], KBt[ks][:, i * 128:(i + 1) * 128],
                               QB[hb][:, qt * 128:qt * 128 + qw], not started[bk], False, [("KBt", ks), ("QB", hb)], pk(bk))
                            started[bk] = True
                            for (ap_, key_) in extra:
                                mm(ps[:, bk, off:off + qw], ident[:], ap_, False, False, ["ident", key_], pk(bk))
                        n01 = len([d for d in ds if d < 2])
                        n24 = len([d for d in ds if d >= 2])
                        act(PTB[pb][:, 0:n01 * 128].rearrange("p (d q) -> p d q", q=128)[:, :, 0:qw],
                            ps[:, bk0, 0:n01 * 128].rearrange("p (d q) -> p d q", q=128)[:, :, 0:qw], AF.Exp,
                            pk(bk0), [("PTB", pb, 0)])
                        if n24:
                            act(PTB[pb][:, 256:256 + n24 * 128].rearrange("p (d q) -> p d q", q=128)[:, :, 0:qw],
                                ps[:, bk1, 0:n24 * 128].rearrange("p (d q) -> p d q", q=128)[:, :, 0:qw], AF.Exp,
                                pk(bk1) + [("b0", h)], [("PTB", pb, 1)], bias=cols[:, C_B0 + h:C_B0 + h + 1])

                    def pvB(h=h, qt=qt, gi=gi, ds=ds, pb=pb, ob=ob, slot_of=slot_of):
                        okey = ("ps", ob)
                        for di, d in enumerate(ds):
                            kt = gi - d
                            ks, gb, i = slot_of[kt]
                            mm(ps[0:qw, ob, 0:129], PTB[pb][:, d * 128:d * 128 + qw], vaug[gb][:, i, 0:129],
                               di == 0, di == len(ds) - 1, [("PTB", pb, 0), ("PTB", pb, 1), ("vaug", gb)], [okey])
                        rc = tcol()
                        fb = next_fb()
                        ov = ps[0:qw, ob, 0:129]
                        S.op("dve", lambda e: e.reciprocal(cols[0:qw, rc:rc + 1], ov[:, 128:129]), [okey], [("col", rc)])
                        vts(Ob[fb][0:qw, 0:128], ov[:, 0:128], cols[0:qw, rc:rc + 1], None, ALU.mult, None,
                            [okey, ("col", rc)], [("Ob", fb)])

                        def post():
                            tp(psb(TPB, 512, 512 + qw), Ob[fb][0:qw, 0:128], ident[0:qw, 0:qw], [("Ob", fb)], pk(TPB))
                            vcopy(oT[:, 8 + h, qt * 128:qt * 128 + qw], psb(TPB, 512, 512 + qw), pk(TPB),
                                  [("oT", 8 + h)])
                        return post

                    stepsB.append(dict(pre=[head_preB] if qt == 0 else [], qk=qkB, pv=pvB, nolook=(qt == 0)))
            run_pipeline(stepsB)

            if DBG['stage'] and DBG['stage'] <= 4:
                S.barrier()
                return
            stepsC = []
            for hc in range(4):
                def head_preC(hc=hc):
                    wt, wk = wload(WQC[hc], 4096, ("WQC", hc))
                    for c in range(2):
                        for k in range(16):
                            mm(ps[:, QPB, 0:nq], wt[:, k * 256 + c * 128:k * 256 + (c + 1) * 128], hT[:, k, 0:nq], k == 0,
                               k == 15, [wk, ("hT", k)], pk(QPB))
                        act(QC[:, c, 0:nq], ps[:, QPB, 0:nq], AF.Identity, pk(QPB), [("QC", c)], scale=1.0 / 16.0)
                    mkk, mvk = memkeys(s, hc)
                    dma("sp", krawC, mkscr[s, hc].rearrange("(t p) d -> p t d", p=128), mkk, ["krawC"])
                    dma("sp", vaugC, mvscr[s, hc].rearrange("(t p) d -> p t d", p=128), mvk, ["vaugC"])
                    for mt in range(2):
                        for c in range(2):
                            idx = mt * 2 + c
                            tp(psb(TPB, idx * 128, (idx + 1) * 128), krawC[:, mt, c * 128:(c + 1) * 128], ident[:],
                               ["krawC"], pk(TPB))
                    for mt in range(2):
                        for c in range(2):
                            idx = mt * 2 + c
                            vcopy(MKT[:, c, mt * 128:(mt + 1) * 128], psb(TPB, idx * 128, (idx + 1) * 128), pk(TPB),
                                  [("MKT", c, mt)])

                for mt in range(2):
                    sc = stepctr[0]
                    stepctr[0] += 1
                    sbk = sc % 2
                    pb = sc % 4

                    def qkC(mt=mt, sbk=sbk, pb=pb):
                        for c in range(2):
                            mm(ps[:, sbk, 0:nq], MKT[:, c, mt * 128:(mt + 1) * 128], QC[:, c, 0:nq], c == 0, c == 1,
                               [("MKT", c, mt), ("QC", c)], pk(sbk))
                        act(PT[pb][:, 0:nq], ps[:, sbk, 0:nq], AF.Exp, pk(sbk), [("PT", pb)])

                    def pvC(hc=hc, mt=mt, pb=pb):
                        for qt in range(nqt):
                            mm(ps[0:qw, 2 + qt, 0:257], PT[pb][:, qt * 128:qt * 128 + qw], vaugC[:, mt, 0:257], mt == 0,
                               mt == 1, [("PT", pb), "vaugC"], pk(2 + qt))
                        if mt == 1:
                            for qt in range(nqt):
                                rc = tcol()
                                fb = next_fb()
                                ov = ps[0:qw, 2 + qt, 0:257]
                                S.op("dve", lambda e, ov=ov, rc=rc: e.reciprocal(cols[0:qw, rc:rc + 1], ov[:, 256:257]),
                                     pk(2 + qt), [("col", rc)])
                                vts(Ob[fb][0:qw, 0:256], ov[:, 0:256], cols[0:qw, rc:rc + 1], None, ALU.mult, None,
                                    pk(2 + qt) + [("col", rc)], [("Ob", fb)])
                                for c in range(2):
                                    tp(psb(TPB, 512 + c * 128, 512 + c * 128 + qw), Ob[fb][0:qw, c * 128:(c + 1) * 128],
                                       ident[0:qw, 0:qw], [("Ob", fb)], pk(TPB))
                                for c in range(2):
                                    vcopy(oT[:, 16 + 2 * hc + c, qt * 128:qt * 128 + qw],
                                          psb(TPB, 512 + c * 128, 512 + c * 128 + qw), pk(TPB), [("oT", 16 + 2 * hc + c)])

                    stepsC.append(dict(pre=[head_preC] if mt == 0 else [], qk=qkC, pv=pvC, nolook=(mt == 0)))
            run_pipeline(stepsC)

            if DBG['stage'] and DBG['stage'] <= 5:
                S.barrier()
                return
            for np_ in range(8):
                for br in range(3):
                    wg, wkg = wload(WMg[np_, br], 4096, ("WMg", np_, br))
                    wbr, wkb = wload(WMb[np_, br], 2048, ("WMb", np_, br))
                    for cc in range(2):
                        n = 2 * np_ + cc
                        bg = nextps([0, 1, 2, 3, 4, 5, 6])
                        bp_ = nextps([0, 1, 2, 3, 4, 5, 6])
                        if bp_ == bg:
                            bp_ = nextps([0, 1, 2, 3, 4, 5, 6])
                        for k in range(16):
                            mm(ps[:, bg, 0:nq], wg[:, k * 256 + cc * 128:k * 256 + (cc + 1) * 128], hT[:, k, 0:nq],
                               k == 0, k == 15, [wkg, ("hT", k)], pk(bg))
                        for k in range(8):
                            mm(ps[:, bp_, 0:nq], wbr[:, k * 256 + cc * 128:k * 256 + (cc + 1) * 128],
                               oT[:, br * 8 + k, 0:nq], k == 0, k == 7, [wkb, ("oT", br * 8 + k)], pk(bp_))
                        si = nextps([0, 1])
                        act(sig[si][:, 0:nq], ps[:, bg, 0:nq], AF.Sigmoid, pk(bg) + [("colblk", C_BG)], [("sig", si)],
                            bias=cols[:, C_BG + br * 16 + n:C_BG + br * 16 + n + 1])
                        acc = (macc, tmpm)[cc]
                        akey = ("macc", cc)
                        if br == 0:
                            vtt(acc[:, 0:nq], sig[si][:, 0:nq], ps[:, bp_, 0:nq], ALU.mult, [("sig", si)] + pk(bp_), [akey])
                        else:
                            vtt(sig[si][:, 0:nq], sig[si][:, 0:nq], ps[:, bp_, 0:nq], ALU.mult, [("sig", si)] + pk(bp_),
                                [("sig", si)])
                            if br == 1:
                                vtt(acc[:, 0:nq], acc[:, 0:nq], sig[si][:, 0:nq], ALU.add, [akey, ("sig", si)], [akey])
                            else:
                                vtt(mT[:, n, 0:nq], acc[:, 0:nq], sig[si][:, 0:nq], ALU.add, [akey, ("sig", si)],
                                    [("mT", n)])

            if DBG['stage'] and DBG['stage'] <= 6:
                S.barrier()
                return
            S.barrier()

            dma("pool", gbc, g_mpost.partition_broadcast(128), [], ["gbc"])
            for nb in range(8):
                wt, wk = wload(WO[nb], 4096, ("WO", nb))
                for tt in range(nqt):
                    b, half = nexthalf()
                    pkey = ("ps", b)
                    for k in range(16):
                        mm(ps[0:qw, b, half:half + 256], mT[:, k, tt * 128:tt * 128 + qw], wt[:, k * 256:(k + 1) * 256], k == 0,
                           k == 15, [wk, ("mT", k)], [pkey])
                    if (nb + tt) % 2 == 0:
                        act(y1t[tt][0:qw, nb * 256:(nb + 1) * 256], ps[0:qw, b, half:half + 256], AF.Identity, [pkey], [("y1", tt, nb)])
                    else:
                        vcopy(y1t[tt][0:qw, nb * 256:(nb + 1) * 256], ps[0:qw, b, half:half + 256], [pkey], [("y1", tt, nb)])
            y1keys = lambda tt: [("y1", tt, nb) for nb in range(8)]
            for tt in range(nqt):
                ssc, rsc = tcol(), tcol()
                act(sqf[0:qw, :], y1t[tt][0:qw, :], AF.Square, y1keys(tt), ["sqf", ("col", ssc)],
                    accum=cols[0:qw, ssc:ssc + 1])
                rstd_from_ss(ssc, D, rsc, qw)
                vstt(y1t[tt][0:qw, :], y1t[tt][0:qw, :], cols[0:qw, rsc:rsc + 1], gbc[0:qw, :], ALU.mult, ALU.mult,
                     y1keys(tt) + [("col", rsc), "gbc"], [("y1", tt, 0)])
                xi = xrr[0] % 2
                xrr[0] += 1
                dma("sp", xt[xi][0:qw, :], xrows(tt), [], ["xt%d" % xi])
                vtt(y1t[tt][0:qw, :], y1t[tt][0:qw, :], xt[xi][0:qw, :], ALU.add, [("y1", tt, 0), "xt%d" % xi],
                    [("y1", tt, 0)])
                dma("pool", x1s[tt, 0:qw, :], y1t[tt][0:qw, :], [("y1", tt, 0)], [("x1s", tt)])
                norm_transpose(None, qw, C_GFPRE, mT, "mT", tt * 128, from_tile=(y1t[tt], ("y1", tt, 0)))

            if DBG['stage'] and DBG['stage'] <= 7:
                S.barrier()
                return
            if not (s == 0 and j == 0):
                flush_deferred(6)
            k0 = 0
            for t3 in range(3):
                nk = THIRDS[t3]
                for pr in range(k0 // 2, (k0 + nk) // 2):
                    wa, wka = wload(WFIa[pr], 4096, ("WFIa", pr))
                    wb, wkb = wload(WFIb[pr], 4096, ("WFIb", pr))
                    for cc in range(2):
                        hc = 2 * pr + cc
                        ba = nextps([0, 1, 2, 3, 4, 5, 6])
                        bb = nextps([0, 1, 2, 3, 4, 5, 6])
                        if bb == ba:
                            bb = nextps([0, 1, 2, 3, 4, 5, 6])
                        for k in range(16):
                            mm(ps[:, ba, 0:nq], wa[:, k * 256 + cc * 128:k * 256 + (cc + 1) * 128], mT[:, k, 0:nq],
                               k == 0, k == 15, [wka, ("mT", k)], pk(ba))
                        for k in range(16):
                            mm(ps[:, bb, 0:nq], wb[:, k * 256 + cc * 128:k * 256 + (cc + 1) * 128], mT[:, k, 0:nq],
                               k == 0, k == 15, [wkb, ("mT", k)], pk(bb))
                        si = nextps([0, 1])
                        act(sa[si][:, 0:nq], ps[:, ba, 0:nq], AF.Silu, pk(ba), [("sa", si)])
                        vtt(uT[:, hc - k0, 0:nq], sa[si][:, 0:nq], ps[:, bb, 0:nq], ALU.mult, [("sa", si)] + pk(bb),
                            [("uT", hc - k0)])
                for nb in range(8):
                    wt, wk = wload(WFO[t3, nb, :, 0:nk * 256], nk * 256, ("WFO", t3, nb))
                    for tt in range(nqt):
                        b, half = nexthalf()
                        pkey = ("ps", b)
                        for kk in range(nk):
                            mm(ps[0:qw, b, half:half + 256], uT[:, kk, tt * 128:tt * 128 + qw], wt[:, kk * 256:(kk + 1) * 256],
                               kk == 0, kk == nk - 1, [wk, ("uT", kk)], [pkey])
                        dst = y1t[tt][0:qw, nb * 256:(nb + 1) * 256]
                        if t3 == 0:
                            act(dst, ps[0:qw, b, half:half + 256], AF.Identity, [pkey],
                                [("y2", tt, nb)] + ([("y1", tt, 0)] if nb == 0 else []))
                        else:
                            vtt(dst, dst, ps[0:qw, b, half:half + 256], ALU.add, [pkey, ("y2", tt, nb)], [("y2", tt, nb)])
                k0 += nk
            dma("pool", gbc, g_fpost.partition_broadcast(128), [], ["gbc"])
            y2keys = lambda tt: [("y2", tt, nb) for nb in range(8)]
            for tt in range(nqt):
                ssc, rsc = tcol(), tcol()
                act(sqf[0:qw, :], y1t[tt][0:qw, :], AF.Square, y2keys(tt), ["sqf", ("col", ssc)],
                    accum=cols[0:qw, ssc:ssc + 1])
                rstd_from_ss(ssc, D, rsc, qw)
                vstt(y1t[tt][0:qw, :], y1t[tt][0:qw, :], cols[0:qw, rsc:rsc + 1], gbc[0:qw, :], ALU.mult, ALU.mult,
                     y2keys(tt) + [("col", rsc), "gbc"], [("y2", tt, 0)])
                xi = xrr[0] % 2
                xrr[0] += 1
                dma("sp", xt[xi][0:qw, :], x1s[tt, 0:qw, :], [("x1s", tt)], ["xt%d" % xi])
                vtt(y1t[tt][0:qw, :], y1t[tt][0:qw, :], xt[xi][0:qw, :], ALU.add, [("y2", tt, 0), "xt%d" % xi],
                    [("y2", tt, 0)])
                ydst = ys[0:qw, :] if sample else yp[s, pos0 + tt * 128:pos0 + tt * 128 + 128, :]
                dma("pool", ydst, y1t[tt][0:qw, :], [("y2", tt, 0)], [("yout", tt)])

            S.barrier()

        S.barrier()
        if DBG['blocks'] is None:
            for s in range(2):
                memory_kv(s)
                for j in range(4):
                    block(s, j, T)
            block(2, 2, NS)
        else:
            for item in DBG['blocks']:
                if item[0] == 'mem':
                    memory_kv(item[1])
                else:
                    block(*item)
        S.emit()
    return nc


_CACHE = {}


def kernel(**inputs):
    f = lambda a: np.ascontiguousarray(np.asarray(a, dtype=np.float32))
    x_prompt = f(inputs["x_prompt"])
    x_sample = f(inputs["x_sample"])
    consts = host_consts()
    shared = dict(
        w_in=f(inputs["w_in"][0]), w_mem=f(inputs["w_mem_kv"][0]),
        w_br0=f(inputs["w_br_a"][0]), w_br1=f(inputs["w_br_b"][0]), w_br2=f(inputs["w_br_c"][0]),
        w_out=f(inputs["w_out"][0]), w_fi=f(inputs["w_ffn_in"][0]), w_fo=f(inputs["w_ffn_out"][0]),
        g_mpre=f(inputs["norm_mix_pre"][0]).reshape(16, 128), g_mem=f(inputs["norm_mem"][0]).reshape(16, 128),
        g_fpre=f(inputs["norm_ffn_pre"][0]).reshape(16, 128), g_mpost=f(inputs["norm_mix_post"][0]).reshape(1, D),
        g_fpost=f(inputs["norm_ffn_post"][0]).reshape(1, D), b_gate=f(inputs["b_gate"][0]).reshape(48, 128),
        lam0=f(inputs["lambda_q1"]).reshape(1, 64), lam1=f(inputs["lambda_k1"]).reshape(1, 64),
        lam2=f(inputs["lambda_q2"]).reshape(1, 64), lam3=f(inputs["lambda_k2"]).reshape(1, 64),
        subln=f(inputs["subln_a"]).reshape(1, 128),
        rbp=np.ascontiguousarray(np.pad(f(inputs["rel_bias_b"][0]), ((0, 0), (128, 0)), mode="edge")),
    )
    shared.update(consts)
    in_maps = []
    for c in range(NCORES):
        m = dict(shared)
        m["xp"] = x_prompt[2 * c:2 * c + 2]
        m["xs"] = x_sample[c]
        m["cak"] = f(inputs["cache_a_k"][0, c]).reshape(PAST, 1024)
        m["cav"] = f(inputs["cache_a_v"][0, c]).reshape(PAST, 1024)
        m["cbk"] = f(inputs["cache_b_k"][0, c]).reshape(512, 1024)
        m["cbv"] = f(inputs["cache_b_v"][0, c]).reshape(512, 1024)
        m["cmk"] = f(inputs["cache_mem_k"][0, c]).reshape(256, 1024)
        m["cmv"] = f(inputs["cache_mem_v"][0, c]).reshape(256, 1024)
        m["memp"] = f(inputs["mem_prompt"][2 * c:2 * c + 2])
        in_maps.append(m)
    if "nc" not in _CACHE:
        _CACHE["nc"] = build()
    res = run_bass_kernel_spmd(_CACHE["nc"], in_maps, core_ids=list(range(NCORES)))
    R = res.results
    cat = lambda k: np.concatenate([np.asarray(r[k]) for r in R], axis=0)
    stk = lambda k: np.stack([np.asarray(r[k]) for r in R], axis=0)
    y_prompt = cat("yp")
    y_sample = stk("ys")
    return (
        y_prompt.astype(np.float32), y_sample.astype(np.float32),
        cat("akp").reshape(1, 16, SEQ, 8, 128), cat("avp").reshape(1, 16, SEQ, 8, 128),
        cat("bkp").reshape(1, 16, 512, 8, 128), cat("bvp").reshape(1, 16, 512, 8, 128),
        cat("mkp").reshape(1, 16, 256, 4, 256), cat("mvp").reshape(1, 16, 256, 4, 256),
        stk("aks").reshape(1, 8, NS, 8, 128), stk("avs").reshape(1, 8, NS, 8, 128),
        stk("bks").reshape(1, 8, NS, 8, 128), stk("bvs").reshape(1, 8, NS, 8, 128),
    )
```

```python
import math
import contextlib
import numpy as np
import concourse.bass as bass
import concourse.mybir as mybir
from concourse.bass_utils import run_bass_kernel_spmd

F32 = mybir.dt.float32
BF16 = mybir.dt.bfloat16
AF = mybir.ActivationFunctionType
ALU = mybir.AluOpType

NCORES = 8
D = 2048
SEQ = 2048
T = 512
PAST = 1024
NS = 16
DFF = 5632
NHC = 44
POSMAX = 2176
EPS = 1e-6
BIG = 30000.0
LAM_INIT = 0.2
THIRDS = (16, 16, 12)
OFF_QA, OFF_KA, OFF_VA, OFF_QB, OFF_KB, OFF_VB, OFF_QC, OFF_G = 0, 1024, 2048, 3072, 4096, 5120, 6144, 7168


class Op:
    __slots__ = ("eng", "fn", "deps", "signal", "sigval", "dma", "dsem", "dval")

    def __init__(self, eng, fn, dma):
        self.eng = eng
        self.fn = fn
        self.deps = []
        self.signal = False
        self.sigval = None
        self.dma = dma
        self.dsem = None
        self.dval = None


class Sched:
    ENGS = ("pe", "act", "dve", "pool", "sp")
    NDS = {"sp": 28, "pool": 24, "act": 16}
    RING_LIMIT = 800

    def __init__(self, nc):
        self.nc = nc
        self.q = {e: [] for e in self.ENGS}
        self.W = {}
        self.R = {}
        self.ds_uses = {k: [0] * n for k, n in self.NDS.items()}
        self.ds_last = {k: [None] * n for k, n in self.NDS.items()}
        self.ds_rr = {k: 0 for k in self.NDS}
        self.all_dma = []
        self.ring = []
        self.ring_sum = 0

    def op(self, eng, fn, reads=(), writes=(), dma=False, ndesc=0):
        o = Op(eng, fn, dma)
        deps = []
        for k in reads:
            w = self.W.get(k)
            if w is not None:
                deps.append(w)
        for k in writes:
            w = self.W.get(k)
            if w is not None:
                deps.append(w)
            r = self.R.get(k)
            if r:
                deps.extend(r[0].values())
                deps.extend(r[1])
        if dma and eng == "pool":
            while self.ring and self.ring_sum + ndesc > self.RING_LIMIT:
                old, nd = self.ring.pop(0)
                self.ring_sum -= nd
                deps.append(old)
            self.ring.append((o, ndesc))
            self.ring_sum += ndesc
        if dma:
            i = self.ds_rr[eng]
            self.ds_rr[eng] = (i + 1) % self.NDS[eng]
            self.ds_uses[eng][i] += 1
            o.dsem = (eng, i)
            o.dval = 16 * self.ds_uses[eng][i]
            if self.ds_last[eng][i] is not None:
                deps.append(self.ds_last[eng][i])
            self.ds_last[eng][i] = o
            self.all_dma.append(o)
        seen = set()
        for d in deps:
            if d is o or id(d) in seen:
                continue
            seen.add(id(d))
            if (not d.dma) and (not dma) and d.eng == eng and eng == "pe":
                continue
            d.signal = True
            o.deps.append(d)
        for k in writes:
            self.W[k] = o
            self.R[k] = ({}, [])
        for k in reads:
            r = self.R.get(k)
            if r is None:
                r = ({}, [])
                self.R[k] = r
            if dma:
                r[1].append(o)
            else:
                r[0][eng] = o
        self.q[eng].append(o)
        return o

    def barrier(self):
        lasts = []
        for e in self.ENGS:
            for o in reversed(self.q[e]):
                if not o.dma and o.fn is not None:
                    lasts.append(o)
                    break
        dm = list(self.all_dma)
        self.all_dma = []
        for e in ("pe", "act", "dve", "pool"):
            o = Op(e, None, False)
            for d in lasts:
                if d.eng == e:
                    continue
                d.signal = True
                o.deps.append(d)
            o.deps.extend(dm)
            self.q[e].append(o)

    def emit(self):
        nc = self.nc
        with contextlib.ExitStack() as st:
            esem = {e: st.enter_context(nc.semaphore("s_" + e)) for e in ("pe", "act", "dve", "pool")}
            dsem = {k: [st.enter_context(nc.semaphore("d%s%d" % (k, i))) for i in range(n)]
                    for k, n in self.NDS.items()}
            for e in self.ENGS:
                c = 0
                for o in self.q[e]:
                    if o.dma or o.fn is None:
                        continue
                    if o.signal:
                        c += 1
                        o.sigval = c
            block = st.enter_context(nc.Block())

            def run(eh, ename):
                seen = {}
                for o in self.q[ename]:
                    need = {}
                    for d in o.deps:
                        if d.dma:
                            key = d.dsem
                            val = d.dval
                        else:
                            key = d.eng
                            val = d.sigval
                        if need.get(key, 0) < val:
                            need[key] = val
                    for key, val in need.items():
                        if seen.get(key, 0) >= val:
                            continue
                        seen[key] = val
                        sem = dsem[key[0]][key[1]] if isinstance(key, tuple) else esem[key]
                        eh.wait_ge(sem, val)
                    if o.fn is None:
                        continue
                    inst = o.fn(eh)
                    if o.dma:
                        inst.then_inc(dsem[o.dsem[0]][o.dsem[1]], 16)
                    elif o.signal:
                        inst.then_inc(esem[ename], 1)
                if ename == "sp":
                    for k, n in self.NDS.items():
                        for i in range(n):
                            v = 16 * self.ds_uses[k][i]
                            if v > 0 and seen.get((k, i), 0) < v:
                                eh.wait_ge(dsem[k][i], v)

            @block.tensor
            def _(e):
                run(e, "pe")

            @block.scalar
            def _(e):
                run(e, "act")

            @block.vector
            def _(e):
                run(e, "dve")

            @block.gpsimd
            def _(e):
                run(e, "pool")

            @block.sync
            def _(e):
                run(e, "sp")


def host_consts():
    slopes = np.array([2.0 ** (-(h + 1)) for h in range(8)], np.float32)
    pos = np.arange(POSMAX)
    kaug = np.stack([np.ones(POSMAX), np.ones(POSMAX), 64.0 * (pos // 64), (pos % 64).astype(np.float64)]).astype(np.float32)
    qbase = np.stack([-64.0 * (pos // 64), -(pos % 64).astype(np.float64), np.ones(POSMAX), np.ones(POSMAX)])
    qaug = (slopes[:, None, None] * qbase[None]).astype(np.float32)
    k = np.arange(128)[:, None]
    q = np.arange(128)[None, :]
    same = (k // 64) == (q // 64)
    M = np.where(same & (k > q), -2.0 * (k - q), 0.0)
    inv = (k // 64) > (q // 64)
    maskA = np.stack([np.where(inv, -BIG, M * s) for s in slopes]).astype(np.float32)
    maskA = np.ascontiguousarray(maskA.transpose(1, 0, 2))
    maskB0 = np.where(inv, -BIG, 0.0).astype(np.float32)
    maskB4 = np.where((k < 64) & (q >= 64), -BIG, 0.0).astype(np.float32)
    ident = np.eye(128, dtype=np.float32)
    J = np.ascontiguousarray(ident[::-1])
    return dict(c_kaug=kaug, c_qaug=qaug, c_maskA=maskA, c_maskB0=maskB0, c_maskB4=maskB4, c_ident=ident, c_J=J)


DBG = {'stage': 0, 'blocks': None, 'skip': ()}


def build():
    nc = bass.Bass("TRN2", target_bir_lowering=False)
    S = Sched(nc)

    def din(name, shape, dt=F32):
        return nc.dram_tensor(name, list(shape), dt, kind="ExternalInput").ap()

    def dout(name, shape):
        return nc.dram_tensor(name, list(shape), F32, kind="ExternalOutput").ap()

    def dint(name, shape, dt):
        return nc.dram_tensor(name, list(shape), dt, kind="Internal").ap()

    xp = din("xp", [2, SEQ, D])
    xs = din("xs", [NS, D])
    cak = din("cak", [PAST, 1024])
    cav = din("cav", [PAST, 1024])
    cbk = din("cbk", [512, 1024])
    cbv = din("cbv", [512, 1024])
    cmk = din("cmk", [256, 1024])
    cmv = din("cmv", [256, 1024])
    memp = din("memp", [2, 256, D])
    w_in = din("w_in", [D, 13312])
    w_mem = din("w_mem", [D, D])
    w_br = [din("w_br%d" % i, [1024, D]) for i in range(3)]
    w_out = din("w_out", [D, D])
    w_fi = din("w_fi", [D, 2 * DFF])
    w_fo = din("w_fo", [DFF, D])
    g_mpre = din("g_mpre", [16, 128])
    g_mem = din("g_mem", [16, 128])
    g_fpre = din("g_fpre", [16, 128])
    g_mpost = din("g_mpost", [1, D])
    g_fpost = din("g_fpost", [1, D])
    b_gate = din("b_gate", [48, 128])
    lam_in = [din("lam%d" % i, [1, 64]) for i in range(4)]
    subln = din("subln", [1, 128])
    rbp = nc.dram_tensor("rbp", [8, 385], F32, kind="ExternalInput")
    c_kaug = din("c_kaug", [4, POSMAX])
    c_qaug = din("c_qaug", [8, 4, POSMAX])
    c_maskA = din("c_maskA", [128, 8, 128])
    c_maskB0 = din("c_maskB0", [128, 128])
    c_maskB4 = din("c_maskB4", [128, 128])
    c_ident = din("c_ident", [128, 128])
    c_J = din("c_J", [128, 128])

    yp = dout("yp", [2, SEQ, D])
    ys = dout("ys", [NS, D])
    akp = dout("akp", [2, SEQ, 1024])
    avp = dout("avp", [2, SEQ, 1024])
    bkp = dout("bkp", [2, 512, 1024])
    bvp = dout("bvp", [2, 512, 1024])
    mkp = dout("mkp", [2, 256, 1024])
    mvp = dout("mvp", [2, 256, 1024])
    aks = dout("aks", [NS, 1024])
    avs = dout("avs", [NS, 1024])
    bks = dout("bks", [NS, 1024])
    bvs = dout("bvs", [NS, 1024])

    kscrA = dint("kscrA", [3, 8, POSMAX, 128], BF16)
    vscrA = dint("vscrA", [3, 8, POSMAX, 130], BF16)
    kscrB = dint("kscrB", [3, 8, POSMAX, 128], BF16)
    vscrB = dint("vscrB", [3, 8, POSMAX, 130], BF16)
    mkscr = dint("mkscr", [3, 4, 256, 256], BF16)
    mvscr = dint("mvscr", [3, 4, 256, 258], BF16)
    x1s = dint("x1s", [4, 128, D], F32)
    WKV = dint("WKV", [16, 128, 4096], BF16)
    WQA = dint("WQA", [8, 128, 2048], BF16)
    WQB = dint("WQB", [8, 128, 2048], BF16)
    WQC = dint("WQC", [4, 128, 4096], BF16)
    WMg = dint("WMg", [8, 3, 128, 4096], BF16)
    WMb = dint("WMb", [8, 3, 128, 2048], BF16)
    WO = dint("WO", [8, 128, 4096], BF16)
    WFIa = dint("WFIa", [22, 128, 4096], BF16)
    WFIb = dint("WFIb", [22, 128, 4096], BF16)
    WFO = dint("WFO", [3, 8, 128, 4096], BF16)
    WMEM = dint("WMEM", [8, 128, 4096], BF16)

    st = contextlib.ExitStack()
    with st:
        def sb(name, shape, dt):
            return st.enter_context(nc.sbuf_tensor(name, list(shape), dt))

        ps = st.enter_context(nc.psum_tensor("ps", [128, 8, 512], F32))

        def psb(bank, lo, hi):
            return ps[:, bank, lo // 2:hi // 2].bitcast(BF16)

        def pk(bank, lo=0, hi=512):
            return [("ps", bank)]

        wbuf = [sb("wbuf%d" % i, [128, 4096], BF16) for i in range(4)]
        xt = [sb("xt%d" % i, [128, D], F32) for i in range(2)]
        xnb = sb("xnb", [128, D], BF16)
        mT = sb("mT", [128, 16, T], BF16)
        ident = sb("ident", [128, 128], BF16)
        identf = sb("identf", [128, 128], F32)
        Jf = sb("Jf", [128, 128], F32)
        maskA = sb("maskA", [128, 8, 128], BF16)
        maskB0 = sb("maskB0", [128, 128], BF16)
        maskB4 = sb("maskB4", [128, 128], BF16)
        TB = sb("TB", [128, 8, 2, 128], BF16)
        gsub = sb("gsub", [128, 128], F32)
        cols = sb("cols", [128, 128], F32)
        K1a = [sb("K1a%d" % g, [128, 512], BF16) for g in range(4)]
        K2a = [sb("K2a%d" % g, [128, 512], BF16) for g in range(4)]
        arena = sb("arena", [128, 44 * 1024], BF16)

        C_GPRE, C_GMEM, C_GFPRE, C_BG = 0, 16, 32, 48
        C_B0, C_LAM, C_NLAM = 96, 104, 105
        C_TMP = 106

        class Carver:
            def __init__(self):
                self.off = 0

            def get(self, nelem, dt, shape=None):
                nb = nelem * (4 if dt == F32 else 2)
                nb = (nb + 63) // 64 * 64
                a = self.off // 2
                self.off += nb
                assert self.off <= 44 * 1024 * 2, "arena overflow %d" % self.off
                v = arena[:, a:a + (nelem * 2 if dt == F32 else nelem)]
                if dt == F32:
                    v = v.bitcast(F32)
                return v

        ca = Carver()
        hT_f = ca.get(16 * T, BF16)
        hT = hT_f.rearrange("p (c t) -> p c t", t=T)
        oT_f = ca.get(24 * T, BF16)
        oT = oT_f.rearrange("p (c t) -> p c t", t=T)
        kraw = [ca.get(512, BF16).rearrange("p (t d) -> p t d", d=128) for _ in range(2)]
        vaug = [ca.get(4 * 130, BF16).rearrange("p (t d) -> p t d", d=130) for _ in range(2)]
        Q1 = [ca.get(512, BF16) for _ in range(2)]
        Q2 = [ca.get(512, BF16) for _ in range(2)]
        QB = [ca.get(512, BF16) for _ in range(2)]
        QC = ca.get(1024, BF16).rearrange("p (c t) -> p c t", t=T)
        KBt = [ca.get(512, BF16) for _ in range(4)]
        PT = [ca.get(512, BF16) for _ in range(4)]
        PTB = [ca.get(640, BF16) for _ in range(2)]
        krawC = ca.get(512, BF16).rearrange("p (t d) -> p t d", d=256)
        vaugC = ca.get(2 * 258, BF16).rearrange("p (t d) -> p t d", d=258)
        MKT = ca.get(512, BF16).rearrange("p (c m) -> p c m", m=256)
        sig = [ca.get(T, F32) for _ in range(2)]
        tmpm = ca.get(T, F32)
        macc = ca.get(T, F32)
        stg32 = [ca.get(256, F32) for _ in range(3)]
        stg16 = [ca.get(256, BF16) for _ in range(2)]
        stgv = [ca.get(2 * 130, BF16).rearrange("p (h d) -> p h d", d=130) for _ in range(2)]
        stgm = [ca.get(258, BF16) for _ in range(2)]
        t2 = [ca.get(128, F32) for _ in range(2)]
        Of = [ca.get(128, F32) for _ in range(2)]
        sqj = ca.get(256, F32)
        Ob = [ca.get(256, BF16) for _ in range(2)]
        Hk = [ca.get(128, F32) for _ in range(2)]
        lamt = [ca.get(64, F32) for _ in range(5)]
        att_end = ca.off
        cf = Carver()
        uT = cf.get(16 * T, BF16).rearrange("p (c t) -> p c t", t=T)
        sa = [cf.get(T, F32) for _ in range(2)]
        sqf = cf.get(D, BF16)
        y1t = [cf.get(D, F32) for _ in range(4)]
        gbc = cf.get(D, F32)

        def mm(out, lhsT, rhs, start, stop, r, w):
            S.op("pe", lambda e: e.matmul(out, lhsT, rhs, start=start, stop=stop, skip_group_check=True), r, w)

        def tp(out, in_, idn, r, w):
            S.op("pe", lambda e: e.transpose(out, in_, idn), list(r) + ["ident"], w)

        def act(out, in_, func, r, w, bias=None, scale=None, accum=None):
            kw = {}
            if bias is not None:
                kw["bias"] = bias
            if scale is not None:
                kw["scale"] = scale
            if accum is not None:
                kw["accum_out"] = accum
            S.op("act", lambda e: e.activation(out=out, in_=in_, func=func, **kw), r, w)

        def ndesc_of(ap):
            dims = [list(x) for x in ap.ap]
            tot = 1
            for st_, n_ in dims:
                tot *= n_
            run = 1
            exp_ = 1
            for st_, n_ in reversed(dims):
                if n_ == 1:
                    continue
                if st_ == exp_:
                    run *= n_
                    exp_ = st_ * n_
                else:
                    break
            return max(1, tot // run)

        def dma(q, out, in_, r, w, slow=False):
            nd = 0
            if q == "pool":
                nd = (2 * max(ndesc_of(out), ndesc_of(in_)) + 15) // 16 + 4
            if slow:
                return S.op(q, lambda e: e.dma_start(out=out, in_=in_, allow_slow_non_contiguous=True), r, w, dma=True,
                            ndesc=nd)
            return S.op(q, lambda e: e.dma_start(out=out, in_=in_), r, w, dma=True, ndesc=nd)

        def vts(out, in0, s1, s2, op0, op1, r, w, eng="dve"):
            if op1 is None:
                S.op(eng, lambda e: e.tensor_scalar(out, in0, s1, None, op0), r, w)
            else:
                S.op(eng, lambda e: e.tensor_scalar(out, in0, s1, s2, op0, op1), r, w)

        def vtt(out, in0, in1, op, r, w, eng="dve"):
            S.op(eng, lambda e: e.tensor_tensor(out, in0, in1, op), r, w)

        def vstt(out, in0, scalar, in1, op0, op1, r, w):
            S.op("dve", lambda e: e.scalar_tensor_tensor(out, in0, scalar, in1, op0, op1), r, w)

        def vcopy(out, in_, r, w, eng="dve"):
            S.op(eng, lambda e: e.tensor_copy(out, in_), r, w)

        col_ctr = [0]

        def tcol():
            c = C_TMP + (col_ctr[0] % 20)
            col_ctr[0] += 1
            return c

        def rstd_from_ss(ss_c, n, r_c, rows, extra_bias=0.0):
            l_c = tcol()
            act(cols[0:rows, l_c:l_c + 1], cols[0:rows, ss_c:ss_c + 1], AF.Ln, [("col", ss_c)], [("col", l_c)],
                bias=EPS, scale=1.0 / n)
            act(cols[0:rows, r_c:r_c + 1], cols[0:rows, l_c:l_c + 1], AF.Exp, [("col", l_c)], [("col", r_c)],
                bias=extra_bias, scale=-0.5)

        wctr = [0]

        def wload(src_ap, nelem, *keys):
            i = wctr[0] % 4
            wctr[0] += 1
            if keys[0] not in wdone:
                for key in keys:
                    for (dst3, src3) in wreg[key]:
                        e0 = dst3.offset - src_ap.offset
                        nk, ncol = dst3.shape[1], dst3.shape[2]
                        dma("pool", wbuf[i][:, e0:e0 + nk * ncol].rearrange("p (k j) -> p k j", j=ncol), src3, [],
                            [("w", i)])
                dma("sp", src_ap, wbuf[i][:, 0:nelem], [("w", i)], list(keys))
                wdone.add(keys[0])
            else:
                dma("sp", wbuf[i][:, 0:nelem], src_ap, list(keys), [("w", i)])
            return wbuf[i], ("w", i)

        dma("pool", ident[:], c_ident, [], ["ident"])
        dma("sp", identf[:], c_ident, [], ["identf"])
        dma("sp", Jf[:], c_J, [], ["Jf"])
        dma("pool", maskA[:], c_maskA, [], ["maskA"])
        dma("pool", maskB0[:], c_maskB0, [], ["maskB0"])
        dma("pool", maskB4[:], c_maskB4, [], ["maskB4"])
        dma("sp", gsub[:], subln.partition_broadcast(128), [], ["gsub"])
        for g in range(4):
            dma("pool", K1a[g][64:68, :], c_kaug[:, g * 512:(g + 1) * 512], [], [("K1aug", g)])
            dma("pool", K2a[g][64:68, :], c_kaug[:, g * 512:(g + 1) * 512], [], [("K2aug", g)])
        for src, nrow, c0 in ((g_mpre, 16, C_GPRE), (g_mem, 16, C_GMEM), (g_fpre, 16, C_GFPRE), (b_gate, 48, C_BG)):
            dma("sp", xt[0][0:nrow, 0:128], src, [], ["xt0"])
            mm(ps[:, 0, 0:nrow], xt[0][0:nrow, 0:128], identf[0:nrow, 0:nrow], True, True, ["xt0", "identf"], pk(0, 0, nrow))
            vcopy(cols[:, c0:c0 + nrow], ps[:, 0, 0:nrow], pk(0, 0, nrow), [("colblk", c0)])
        for h in range(8):
            src = bass.AP(tensor=rbp, offset=h * 385 + 128, ap=[[0, 128], [1, 1]])
            dma("sp", cols[:, C_B0 + h:C_B0 + h + 1], src, [], [("b0", h)])
        for i in range(4):
            dma("sp", lamt[i], lam_in[i].partition_broadcast(128), [], [("lamt", i)])
        c1, c2 = tcol(), tcol()
        vtt(lamt[4], lamt[0], lamt[1], ALU.mult, [("lamt", 0), ("lamt", 1)], [("lamt", 4)])
        S.op("dve", lambda e: e.tensor_reduce(cols[:, c1:c1 + 1], lamt[4], mybir.AxisListType.X, ALU.add), [("lamt", 4)], [("col", c1)])
        vtt(lamt[4], lamt[2], lamt[3], ALU.mult, [("lamt", 2), ("lamt", 3)], [("lamt", 4)])
        S.op("dve", lambda e: e.tensor_reduce(cols[:, c2:c2 + 1], lamt[4], mybir.AxisListType.X, ALU.add), [("lamt", 4)], [("col", c2)])
        act(cols[:, c1:c1 + 1], cols[:, c1:c1 + 1], AF.Exp, [("col", c1)], [("col", c1)])
        act(cols[:, c2:c2 + 1], cols[:, c2:c2 + 1], AF.Exp, [("col", c2)], [("col", c2)])
        vtt(cols[:, C_LAM:C_LAM + 1], cols[:, c1:c1 + 1], cols[:, c2:c2 + 1], ALU.subtract, [("col", c1), ("col", c2)], ["lam"])
        vts(cols[:, C_NLAM:C_NLAM + 1], cols[:, C_LAM:C_LAM + 1], LAM_INIT, -1.0, ALU.add, ALU.mult, ["lam"], ["nlam"])
        for h in range(8):
            for d in range(2):
                hk = Hk[(h * 2 + d) % 2]
                hkey = ("Hk", (h * 2 + d) % 2)
                src = bass.AP(tensor=rbp, offset=h * 385 + 129 - 128 * d, ap=[[1, 128], [1, 128]])
                dma("sp", hk, src, [], [hkey])
                bk = 1 + (h * 2 + d) % 2
                mm(ps[:, bk, 0:128], hk, Jf[:], True, True, [hkey, "Jf"], pk(bk, 0, 128))
                vcopy(TB[:, h, d, :], ps[:, bk, 0:128], pk(bk, 0, 128), [("TB", h, d)])

        wreg = {}
        wdone = set()

        def conv(dst, src, key):
            wreg.setdefault(key, []).append((dst, src))

        def kview(wap, r0, nk, c0, ncol):
            return wap[r0:r0 + nk * 128, c0:c0 + ncol].rearrange("(k p) j -> p k j", p=128)

        def d3(dst, nk, ncol, e0=0):
            return dst[:, e0:e0 + nk * ncol].rearrange("p (k j) -> p k j", j=ncol)

        kvoffs = [OFF_KA, OFF_VA, OFF_KB, OFF_VB]
        for nb in range(8):
            conv(d3(WMEM[nb], 16, 256), kview(w_mem, 0, 16, nb * 256, 256), ("WMEM", nb))
        for nb in range(16):
            c0 = kvoffs[nb // 4] + (nb % 4) * 256
            conv(d3(WKV[nb], 16, 256), kview(w_in, 0, 16, c0, 256), ("WKV", nb))
        for h in range(8):
            conv(d3(WQA[h], 16, 128), kview(w_in, 0, 16, OFF_QA + h * 128, 128), ("WQA", h))
        for h in range(8):
            conv(d3(WQB[h], 16, 128), kview(w_in, 0, 16, OFF_QB + h * 128, 128), ("WQB", h))
        for h in range(4):
            conv(d3(WQC[h], 16, 256), kview(w_in, 0, 16, OFF_QC + h * 256, 256), ("WQC", h))
        for np_ in range(8):
            for br in range(3):
                conv(d3(WMg[np_, br], 16, 256), kview(w_in, 0, 16, OFF_G + br * 2048 + np_ * 256, 256), ("WMg", np_, br))
                conv(d3(WMb[np_, br], 8, 256), kview(w_br[br], 0, 8, np_ * 256, 256), ("WMb", np_, br))
        for nb in range(8):
            conv(d3(WO[nb], 16, 256), kview(w_out, 0, 16, nb * 256, 256), ("WO", nb))
        for pr in range(22):
            conv(d3(WFIa[pr], 16, 256), kview(w_fi, 0, 16, pr * 256, 256), ("WFIa", pr))
            conv(d3(WFIb[pr], 16, 256), kview(w_fi, 0, 16, DFF + pr * 256, 256), ("WFIb", pr))
        k0_ = 0
        for t3 in range(3):
            nk = THIRDS[t3]
            for nb in range(8):
                conv(d3(WFO[t3, nb], nk, 256), kview(w_fo, k0_ * 128, nk, nb * 256, 256), ("WFO", t3, nb))
            k0_ += nk

        vones = wbuf[3][:, 0:8 * 130].rearrange("p (h d) -> p h d", d=130)
        S.op("pool", lambda e: e.memset(vones, 1.0), [], [("w", 3)])
        S.op("pool", lambda e: e.memset(xnb[:], 0.0), [], ["xnb"])
        for kind_, (scr, wd) in enumerate(((kscrA, 128), (vscrA, 130), (kscrB, 128), (vscrB, 130)) if 'pad' not in DBG['skip'] else ()):
            dma("pool", scr[2, :, 1024:1152, :].rearrange("h p d -> p h d"),
                xnb[:, 0:8 * wd].rearrange("p (h d) -> p h d", d=wd), ["xnb"], [("scrpad", kind_)])
        for kt in range(8 if 'cache' not in DBG['skip'] else 0):
            dma("pool", kscrA[2, :, kt * 128:(kt + 1) * 128, :].rearrange("h p d -> p h d"),
                cak[kt * 128:(kt + 1) * 128, :].rearrange("p (h d) -> p h d", d=128), [], [("kvA", "k", 2, kt)])
            dma("pool", vscrA[2, :, kt * 128:(kt + 1) * 128, 0:128].rearrange("h p d -> p h d"),
                cav[kt * 128:(kt + 1) * 128, :].rearrange("p (h d) -> p h d", d=128), [], [("kvA", "v", 2, kt)])
            dma("pool", vscrA[2, :, kt * 128:(kt + 1) * 128, 128:130].rearrange("h p d -> p h d"),
                vones[:, :, 128:130], [("w", 3)], [("kvA", "v1", 2, kt)], slow=True)
        for kt in range(4, 8 if 'cache' not in DBG['skip'] else 4):
            r0 = (kt - 4) * 128
            dma("pool", kscrB[2, :, kt * 128:(kt + 1) * 128, :].rearrange("h p d -> p h d"),
                cbk[r0:r0 + 128, :].rearrange("p (h d) -> p h d", d=128), [], [("kvB", "k", 2, kt)])
            dma("pool", vscrB[2, :, kt * 128:(kt + 1) * 128, 0:128].rearrange("h p d -> p h d"),
                cbv[r0:r0 + 128, :].rearrange("p (h d) -> p h d", d=128), [], [("kvB", "v", 2, kt)])
            dma("pool", vscrB[2, :, kt * 128:(kt + 1) * 128, 128:130].rearrange("h p d -> p h d"),
                vones[:, :, 128:130], [("w", 3)], [("kvB", "v1", 2, kt)], slow=True)
        for mt in range(2 if 'cache' not in DBG['skip'] else 0):
            dma("pool", mkscr[2, :, mt * 128:(mt + 1) * 128, :].rearrange("h p d -> p h d"),
                cmk[mt * 128:(mt + 1) * 128, :].rearrange("p (h d) -> p h d", d=256), [], [("kvM", "k", 2, mt)])
            dma("pool", mvscr[2, :, mt * 128:(mt + 1) * 128, 0:256].rearrange("h p d -> p h d"),
                cmv[mt * 128:(mt + 1) * 128, :].rearrange("p (h d) -> p h d", d=256), [], [("kvM", "v", 2, mt)])
            dma("pool", mvscr[2, :, mt * 128:(mt + 1) * 128, 256:258].rearrange("h p d -> p h d"),
                vones[:, 0:4, 128:130], [("w", 3)], [("kvM", "v1", 2, mt)], slow=True)
        for i in range(2):
            S.op("pool", lambda e, i=i: e.memset(stgv[i][:, :, 128:130], 1.0), [], [("stgv1", i)])
            S.op("pool", lambda e, i=i: e.memset(stgm[i][:, 256:258], 1.0), [], [("stgm1", i)])

        psrr = [0]

        def nextps(banks):
            b = banks[psrr[0] % len(banks)]
            psrr[0] += 1
            return b

        hrr = [0]

        def nexthalf():
            v = hrr[0] % 6
            hrr[0] += 1
            return v, 0

        xrr = [0]

        def norm_transpose(src_rows, rows, gcol0, dstT, dstkey, col0, srckey=None, from_tile=None):
            if from_tile is None:
                xi = xrr[0] % 2
                xrr[0] += 1
                xtile = xt[xi]
                xkey = "xt%d" % xi
                dma("sp", xtile[0:rows, :], src_rows, [srckey] if srckey else [], [xkey])
            else:
                xtile, xkey = from_tile
            ss_c, r_c = tcol(), tcol()
            act(xnb[0:rows, :], xtile[0:rows, :], AF.Square, [xkey], ["xnb", ("col", ss_c)],
                accum=cols[0:rows, ss_c:ss_c + 1])
            if 'nt1' in DBG['skip']:
                return
            rstd_from_ss(ss_c, D, r_c, rows)
            if 'nt2' in DBG['skip']:
                return
            vts(xnb[0:rows, :], xtile[0:rows, :], cols[0:rows, r_c:r_c + 1], None, ALU.mult, None,
                [xkey, ("col", r_c)], ["xnb"])
            if 'nt3' in DBG['skip']:
                return
            for c4 in range(4):
                half = c4 % 2
                for i in range(4):
                    c = c4 * 4 + i
                    tp(psb(6 + half, i * 128, i * 128 + rows), xnb[0:rows, c * 128:(c + 1) * 128],
                       ident[0:rows, 0:rows], ["xnb"], [("ps", 6 + half)])
                for i in range(4):
                    c = c4 * 4 + i
                    vts(dstT[:, c, col0:col0 + rows], psb(6 + half, i * 128, i * 128 + rows),
                        cols[:, gcol0 + c:gcol0 + c + 1], None, ALU.mult, None,
                        [("ps", 6 + half), ("colblk", gcol0)], [(dstkey, c)])

        def memory_kv(s):
            for tt in range(2):
                norm_transpose(memp[s, tt * 128:(tt + 1) * 128, :], 128, C_GMEM, hT, "hT", tt * 128)
            S.barrier()
            if 'memA' in DBG['skip']:
                return
            for nb in range(8):
                wt, wk = wload(WMEM[nb], 4096, ("WMEM", nb))
                isv = nb >= 4
                hh = (nb % 4)
                for tt in range(2):
                    b, half = nexthalf()
                    pkey = ("ps", b)
                    for k in range(16):
                        mm(ps[:, b, half:half + 256], hT[:, k, tt * 128:(tt + 1) * 128], wt[:, k * 256:(k + 1) * 256],
                           k == 0, k == 15, [wk, ("hT", k)], [pkey])
                    si = nextps([0, 1, 2])
                    act(stg32[si], ps[:, b, half:half + 256], AF.Identity, [pkey], [("stg32", si)])
                    dst = (mvp if isv else mkp)[s, tt * 128:(tt + 1) * 128, hh * 256:(hh + 1) * 256]
                    if 'memB' not in DBG['skip']:
                        dma("act", dst, stg32[si], [("stg32", si)], [])
                    if 'memC' in DBG['skip']:
                        continue
                    mi = nextps([0, 1])
                    vcopy(stgm[mi][:, 0:256], stg32[si], [("stg32", si)], [("stgm", mi)])
                    if 'memD' in DBG['skip']:
                        continue
                    if isv:
                        dma("act", mvscr[s, hh, tt * 128:(tt + 1) * 128, :], stgm[mi][:, 0:258],
                            [("stgm", mi), ("stgm1", mi)], [("kvM", "v", s, tt, hh)])
                    else:
                        dma("act", mkscr[s, hh, tt * 128:(tt + 1) * 128, :], stgm[mi][:, 0:256],
                            [("stgm", mi)], [("kvM", "k", s, tt, hh)])

        def memkeys(s, hc):
            if s == 2:
                return [("kvM", "k", 2, 0), ("kvM", "k", 2, 1)], [("kvM", "v", 2, 0), ("kvM", "v", 2, 1),
                                                                   ("kvM", "v1", 2, 0), ("kvM", "v1", 2, 1)]
            return [("kvM", "k", s, 0, hc), ("kvM", "k", s, 1, hc)], [("kvM", "v", s, 0, hc), ("kvM", "v", s, 1, hc)]

        def block(s, j, nq):
            sample = (s == 2)
            nqt = (nq + 127) // 128
            qw = min(128, nq)
            pos0 = j * T
            xsrc = xs if sample else xp[s, pos0:pos0 + T, :]
            kmin = 4 if sample else 0

            def xrows(tt):
                return xsrc[tt * 128:tt * 128 + qw, :]

            for tt in range(nqt):
                norm_transpose(xrows(tt), qw, C_GPRE, hT, "hT", tt * 128)
            S.barrier()

            if DBG['stage'] and DBG['stage'] <= 1:
                S.barrier()
                return
            for nb in range(16):
                wt, wk = wload(WKV[nb], 4096, ("WKV", nb))
                kind = nb // 4
                h0 = (nb % 4) * 2
                for tt in range(nqt):
                    b, half = nexthalf()
                    pso = ps[0:qw, b, half:half + 256]
                    pkey = ("ps", b)
                    for k in range(16):
                        mm(pso, hT[:, k, tt * 128:tt * 128 + qw], wt[:, k * 256:(k + 1) * 256],
                           k == 0, k == 15, [wk, ("hT", k)], [pkey])
                    r0 = pos0 + tt * 128
                    odst = None
                    if sample:
                        odst = (aks, avs, bks, bvs)[kind][0:qw, h0 * 128:h0 * 128 + 256]
                    elif kind < 2:
                        odst = (akp, avp)[kind][s, r0:r0 + 128, h0 * 128:h0 * 128 + 256]
                    elif j == 3:
                        odst = (bkp, bvp)[kind - 2][s, tt * 128:(tt + 1) * 128, h0 * 128:h0 * 128 + 256]
                    esrc, ekey = pso, pkey
                    if odst is not None:
                        si = nextps([0, 1, 2])
                        act(stg32[si][0:qw, :], pso, AF.Identity, [pkey], [("stg32", si)])
                        dma("act", odst, stg32[si][0:qw, :], [("stg32", si)], [])
                        esrc, ekey = stg32[si][0:qw, :], ("stg32", si)
                    kt = r0 // 128
                    if kind in (0, 2):
                        mi = nextps([0, 1])
                        vcopy(stg16[mi][0:qw, :], esrc, [ekey], [("stg16", mi)])
                        scr = kscrA if kind == 0 else kscrB
                        dma("act", scr[s, h0:h0 + 2, r0:r0 + qw, :].rearrange("h p d -> p h d"),
                            stg16[mi][0:qw, :].rearrange("p (h d) -> p h d", d=128),
                            [("stg16", mi)] + ([("scrpad", kind)] if sample else []), [("kv", kind, s, kt, h0)])
                    else:
                        mi = nextps([0, 1])
                        vcopy(stgv[mi][0:qw, :, 0:128], esrc.rearrange("p (h d) -> p h d", d=128), [ekey], [("stgv", mi)])
                        scr = vscrA if kind == 1 else vscrB
                        dma("act", scr[s, h0:h0 + 2, r0:r0 + qw, :].rearrange("h p d -> p h d"),
                            stgv[mi][0:qw, :, :],
                            [("stgv", mi), ("stgv1", mi)] + ([("scrpad", kind)] if sample else []),
                            [("kv", kind, s, kt, h0)])

            def kvkeys(branch, which, kts, h):
                out = []
                kind = (0 if branch == "A" else 2) + which
                for kt in kts:
                    if sample and kt < 8:
                        nm = "kvA" if branch == "A" else "kvB"
                        out.append((nm, "kv"[which], 2, kt))
                        if which == 1:
                            out.append((nm, "v1", 2, kt))
                    else:
                        out.append(("kv", kind, s, kt, (h // 2) * 2))
                        if sample:
                            out.append(("scrpad", kind))
                return out

            if DBG['stage'] and DBG['stage'] <= 2:
                S.barrier()
                return
            def run_pipeline(steps, depth=1, post_delay=1):
                n = len(steps)
                issued = 0
                pending = []
                for i in range(n):
                    while issued < n and issued <= i + depth:
                        if issued > i and steps[issued].get("nolook"):
                            break
                        for f in steps[issued]["pre"]:
                            f()
                        steps[issued]["qk"]()
                        issued += 1
                    post = steps[i]["pv"]()
                    pending = [(d - 1, f) for (d, f) in pending]
                    while pending and pending[0][0] <= 0:
                        pending.pop(0)[1]()
                    if post is not None:
                        pending.append((post_delay, post))
                for (_, f) in pending:
                    f()

            fbc_ = [0]

            def next_fb():
                v = fbc_[0] % 2
                fbc_[0] += 1
                return v

            gbc_ = [0]

            def next_gb():
                v = gbc_[0] % 2
                gbc_[0] += 1
                return v

            ksc_ = [0]

            def next_ks():
                v = ksc_[0] % 4
                ksc_[0] += 1
                return v

            TPB = 7
            QPB = 7
            SBA = (0, 1, 6)
            stepctr = [0]

            kt_max = 4 * j + nqt - 1
            stepsA = []
            for h in range(8):
                hb = h % 2

                def head_pre(h=h, hb=hb):
                    wt, wk = wload(WQA[h], 2048, ("WQA", h))
                    for k in range(16):
                        mm(ps[:, QPB, 0:nq], wt[:, k * 128:(k + 1) * 128], hT[:, k, 0:nq], k == 0, k == 15,
                           [wk, ("hT", k)], pk(QPB))
                    act(Q1[hb][0:64, 0:nq], ps[0:64, QPB, 0:nq], AF.Identity, pk(QPB), [("Q1", hb)], scale=0.125)
                    act(Q2[hb][0:64, 0:nq], ps[64:128, QPB, 0:nq], AF.Identity, pk(QPB), [("Q2", hb)], scale=0.125)
                    dma("pool", Q1[hb][64:68, 0:nq], c_qaug[h, :, pos0:pos0 + nq], [], [("Q1a", hb)])
                    dma("pool", Q2[hb][64:68, 0:nq], c_qaug[h, :, pos0:pos0 + nq], [], [("Q2a", hb)])

                first_of_head = True
                for g in range(kt_max // 4 + 1):
                    kts = list(range(4 * g, min(4 * g + 3, kt_max) + 1))
                    n_kt = len(kts)
                    gb = next_gb()

                    def group_pre(h=h, g=g, kts=kts, n_kt=n_kt, gb=gb):
                        dma("sp", kraw[gb][:, 0:n_kt, :],
                            kscrA[s, h, 4 * g * 128:(4 * g + n_kt) * 128, :].rearrange("(t p) d -> p t d", p=128),
                            kvkeys("A", 0, kts, h), [("kraw", gb)])
                        dma("sp", vaug[gb][:, 0:n_kt, :],
                            vscrA[s, h, 4 * g * 128:(4 * g + n_kt) * 128, :].rearrange("(t p) d -> p t d", p=128),
                            kvkeys("A", 1, kts, h), [("vaug", gb)])
                        for i in range(n_kt):
                            tp(psb(TPB, i * 128, (i + 1) * 128), kraw[gb][:, i, :], ident[:], [("kraw", gb)], pk(TPB))
                        act(K1a[g][0:64, 0:n_kt * 128], psb(TPB, 0, n_kt * 128)[0:64, :], AF.Identity, pk(TPB),
                            [("K1", g)])
                        act(K2a[g][0:64, 0:n_kt * 128], psb(TPB, 0, n_kt * 128)[64:128, :], AF.Identity, pk(TPB),
                            [("K2", g)])

                    first_of_group = True
                    for i, kt in enumerate(kts):
                        t = kt - 4 * j
                        diag = t >= 0
                        c0 = t * 128 if diag else 0
                        for m in range(2):
                            sc = stepctr[0]
                            stepctr[0] += 1
                            sbk = SBA[sc % 3]
                            pb = sc % 4
                            pre = []
                            if first_of_head:
                                pre.append(head_pre)
                                first_of_head = False
                            if first_of_group:
                                pre.append(group_pre)
                                first_of_group = False

                            def qk(h=h, hb=hb, g=g, i=i, m=m, diag=diag, c0=c0, sbk=sbk, pb=pb):
                                Km = (K1a, K2a)[m][g]
                                Qm = (Q1, Q2)[m][hb]
                                kkeys = [("K1", g), ("K1aug", g)] if m == 0 else [("K2", g), ("K2aug", g)]
                                qkeys = [("Q1", hb), ("Q1a", hb)] if m == 0 else [("Q2", hb), ("Q2a", hb)]
                                mm(ps[:, sbk, c0:nq], Km[0:68, i * 128:(i + 1) * 128], Qm[0:68, c0:nq], True, not diag,
                                   kkeys + qkeys, pk(sbk))
                                if diag:
                                    mm(ps[:, sbk, c0:c0 + qw], ident[:], maskA[:, h, 0:qw], False, True,
                                       ["ident", "maskA"], pk(sbk))
                                act(PT[pb][:, c0:nq], ps[:, sbk, c0:nq], AF.Exp, pk(sbk), [("PT", pb)])

                            def pv(h=h, gb=gb, i=i, kt=kt, m=m, t=t, diag=diag, pb=pb):
                                for qt in range(max(t, 0), nqt):
                                    bank = 2 + qt
                                    off = m * 256
                                    mm(ps[0:qw, bank, off:off + 129], PT[pb][:, qt * 128:qt * 128 + qw],
                                       vaug[gb][:, i, 0:129], (kt == 0) and m == 0, kt == 4 * j + qt,
                                       [("PT", pb), ("vaug", gb)], pk(bank))
                                if m == 1 and diag and t < nqt:
                                    qt = t
                                    o1 = ps[0:qw, 2 + qt, 0:129]
                                    o2 = ps[0:qw, 2 + qt, 256:385]
                                    kk_ = pk(2 + qt)
                                    fb = next_fb()
                                    r1, r2, ssc, rsc = tcol(), tcol(), tcol(), tcol()
                                    S.op("dve", lambda e: e.reciprocal(cols[0:qw, r1:r1 + 1], o1[:, 128:129]), kk_,
                                         [("col", r1)])
                                    S.op("dve", lambda e: e.reciprocal(cols[0:qw, r2:r2 + 1], o2[:, 128:129]), kk_,
                                         [("col", r2)])
                                    vtt(cols[0:qw, r2:r2 + 1], cols[0:qw, r2:r2 + 1], cols[0:qw, C_NLAM:C_NLAM + 1],
                                        ALU.mult, [("col", r2), "nlam"], [("col", r2)])
                                    vts(t2[fb][0:qw, :], o2[:, 0:128], cols[0:qw, r2:r2 + 1], None, ALU.mult, None,
                                        kk_ + [("col", r2)], [("t2", fb)])
                                    vstt(Of[fb][0:qw, :], o1[:, 0:128], cols[0:qw, r1:r1 + 1], t2[fb][0:qw, :], ALU.mult,
                                         ALU.add, kk_ + [("col", r1), ("t2", fb)], [("Of", fb)])
                                    act(sqj[0:qw, 0:128], Of[fb][0:qw, :], AF.Square, [("Of", fb)], ["sqj", ("col", ssc)],
                                        accum=cols[0:qw, ssc:ssc + 1])
                                    rstd_from_ss(ssc, 128, rsc, qw, extra_bias=math.log(1.0 - LAM_INIT))
                                    vstt(Ob[fb][0:qw, 0:128], Of[fb][0:qw, :], cols[0:qw, rsc:rsc + 1], gsub[0:qw, :],
                                         ALU.mult, ALU.mult, [("Of", fb), ("col", rsc), "gsub"], [("Ob", fb)])

                                    def post():
                                        tp(psb(TPB, 512, 512 + qw), Ob[fb][0:qw, 0:128], ident[0:qw, 0:qw], [("Ob", fb)],
                                           pk(TPB))
                                        vcopy(oT[:, h, qt * 128:qt * 128 + qw], psb(TPB, 512, 512 + qw), pk(TPB),
                                              [("oT", h)])
                                    return post
                                return None

                            stepsA.append(dict(pre=pre, qk=qk, pv=pv))
            run_pipeline(stepsA, depth=2, post_delay=3)

            if DBG['stage'] and DBG['stage'] <= 3:
                S.barrier()
                return
            stepsB = []
            for h in range(8):
                hb = h % 2
                slot_of = {}
                groups = []
                for g in (j - 1, j):
                    kts = [kt for kt in range(4 * g, 4 * g + 4) if kmin <= kt <= 4 * j + nqt - 1 and kt >= 0]
                    if not kts:
                        continue
                    gb = next_gb()
                    ks = next_ks()
                    groups.append((kts, gb, ks))
                    for i, kt in enumerate(kts):
                        slot_of[kt] = (ks, gb, i)

                def head_preB(h=h, hb=hb, groups=groups):
                    wt, wk = wload(WQB[h], 2048, ("WQB", h))
                    for k in range(16):
                        mm(ps[:, QPB, 0:nq], wt[:, k * 128:(k + 1) * 128], hT[:, k, 0:nq], k == 0, k == 15,
                           [wk, ("hT", k)], pk(QPB))
                    act(QB[hb][:, 0:nq], ps[:, QPB, 0:nq], AF.Identity, pk(QPB), [("QB", hb)], scale=128.0 ** -0.5)
                    for (kts, gb, ks) in groups:
                        n_kt = len(kts)
                        k0 = kts[0]
                        dma("sp", kraw[gb][:, 0:n_kt, :],
                            kscrB[s, h, k0 * 128:(k0 + n_kt) * 128, :].rearrange("(t p) d -> p t d", p=128),
                            kvkeys("B", 0, kts, h), [("kraw", gb)])
                        dma("sp", vaug[gb][:, 0:n_kt, :],
                            vscrB[s, h, k0 * 128:(k0 + n_kt) * 128, :].rearrange("(t p) d -> p t d", p=128),
                            kvkeys("B", 1, kts, h), [("vaug", gb)])
                        for i in range(n_kt):
                            tp(psb(TPB, i * 128, (i + 1) * 128), kraw[gb][:, i, :], ident[:], [("kraw", gb)], pk(TPB))
                        vcopy(KBt[ks][:, 0:n_kt * 128], psb(TPB, 0, n_kt * 128), pk(TPB), [("KBt", ks)])

                for qt in range(nqt):
                    gi = 4 * j + qt
                    ds = [d for d in range(5) if gi - d >= kmin and gi - d >= 0]
                    sc = stepctr[0]
                    stepctr[0] += 1
                    pair = sc % 2
                    pb = sc % 2
                    ob = 4 + (sc % 2)

                    def qkB(h=h, hb=hb, qt=qt, gi=gi, ds=ds, pair=pair, pb=pb, slot_of=slot_of):
                        bk0, bk1 = 2 * pair, 2 * pair + 1
                        started = {bk0: False, bk1: False}
                        for d in ds:
                            kt = gi - d
                            ks, gb, i = slot_of[kt]
                            bk = bk0 if d < 2 else bk1
                            off = (d if d < 2 else d - 2) * 128
                            extra = []
                            if d == 0:
                                extra = [(TB[:, h, 0, 0:qw], ("TB", h, 0)), (maskB0[:, 0:qw], "maskB0")]
                            elif d == 1:
                                extra = [(TB[:, h, 1, 0:qw], ("TB", h, 1))]
                            elif d == 4:
                                extra = [(maskB4[:, 0:qw], "maskB4")]
                            mm(ps[:, bk, off:off + qw], KBt[ks][:, i * 128:(i + 1) * 128],
                               QB[hb][:, qt * 128:qt * 128 + qw], not started[bk], False, [("KBt", ks), ("QB", hb)], pk(bk))
                            started[bk] = True
                            for (ap_, key_) in extra:
                                mm(ps[:, bk, off:off + qw], ident[:], ap_, False, False, ["ident", key_], pk(bk))
                        n01 = len([d for d in ds if d < 2])
                        n24 = len([d for d in ds if d >= 2])
                        act(PTB[pb][:, 0:n01 * 128].rearrange("p (d q) -> p d q", q=128)[:, :, 0:qw],
                            ps[:, bk0, 0:n01 * 128].rearrange("p (d q) -> p d q", q=128)[:, :, 0:qw], AF.Exp,
                            pk(bk0), [("PTB", pb, 0)])
                        if n24:
                            act(PTB[pb][:, 256:256 + n24 * 128].rearrange("p (d q) -> p d q", q=128)[:, :, 0:qw],
                                ps[:, bk1, 0:n24 * 128].rearrange("p (d q) -> p d q", q=128)[:, :, 0:qw], AF.Exp,
                                pk(bk1) + [("b0", h)], [("PTB", pb, 1)], bias=cols[:, C_B0 + h:C_B0 + h + 1])

                    def pvB(h=h, qt=qt, gi=gi, ds=ds, pb=pb, ob=ob, slot_of=slot_of):
                        okey = ("ps", ob)
                        for di, d in enumerate(ds):
                            kt = gi - d
                            ks, gb, i = slot_of[kt]
                            mm(ps[0:qw, ob, 0:129], PTB[pb][:, d * 128:d * 128 + qw], vaug[gb][:, i, 0:129],
                               di == 0, di == len(ds) - 1, [("PTB", pb, 0), ("PTB", pb, 1), ("vaug", gb)], [okey])
                        rc = tcol()
                        fb = next_fb()
                        ov = ps[0:qw, ob, 0:129]
                        S.op("dve", lambda e: e.reciprocal(cols[0:qw, rc:rc + 1], ov[:, 128:129]), [okey], [("col", rc)])
                        vts(Ob[fb][0:qw, 0:128], ov[:, 0:128], cols[0:qw, rc:rc + 1], None, ALU.mult, None,
                            [okey, ("col", rc)], [("Ob", fb)])

                        def post():
                            tp(psb(TPB, 512, 512 + qw), Ob[fb][0:qw, 0:128], ident[0:qw, 0:qw], [("Ob", fb)], pk(TPB))
                            vcopy(oT[:, 8 + h, qt * 128:qt * 128 + qw], psb(TPB, 512, 512 + qw), pk(TPB),
                                  [("oT", 8 + h)])
                        return post

                    stepsB.append(dict(pre=[head_preB] if qt == 0 else [], qk=qkB, pv=pvB, nolook=(qt == 0)))
            run_pipeline(stepsB)

            if DBG['stage'] and DBG['stage'] <= 4:
                S.barrier()
                return
            stepsC = []
            for hc in range(4):
                def head_preC(hc=hc):
                    wt, wk = wload(WQC[hc], 4096, ("WQC", hc))
                    for c in range(2):
                        for k in range(16):
                            mm(ps[:, QPB, 0:nq], wt[:, k * 256 + c * 128:k * 256 + (c + 1) * 128], hT[:, k, 0:nq], k == 0,
                               k == 15, [wk, ("hT", k)], pk(QPB))
                        act(QC[:, c, 0:nq], ps[:, QPB, 0:nq], AF.Identity, pk(QPB), [("QC", c)], scale=1.0 / 16.0)
                    mkk, mvk = memkeys(s, hc)
                    dma("sp", krawC, mkscr[s, hc].rearrange("(t p) d -> p t d", p=128), mkk, ["krawC"])
                    dma("sp", vaugC, mvscr[s, hc].rearrange("(t p) d -> p t d", p=128), mvk, ["vaugC"])
                    for mt in range(2):
                        for c in range(2):
                            idx = mt * 2 + c
                            tp(psb(TPB, idx * 128, (idx + 1) * 128), krawC[:, mt, c * 128:(c + 1) * 128], ident[:],
                               ["krawC"], pk(TPB))
                    for mt in range(2):
                        for c in range(2):
                            idx = mt * 2 + c
                            vcopy(MKT[:, c, mt * 128:(mt + 1) * 128], psb(TPB, idx * 128, (idx + 1) * 128), pk(TPB),
                                  [("MKT", c, mt)])

                for mt in range(2):
                    sc = stepctr[0]
                    stepctr[0] += 1
                    sbk = sc % 2
                    pb = sc % 4

                    def qkC(mt=mt, sbk=sbk, pb=pb):
                        for c in range(2):
                            mm(ps[:, sbk, 0:nq], MKT[:, c, mt * 128:(mt + 1) * 128], QC[:, c, 0:nq], c == 0, c == 1,
                               [("MKT", c, mt), ("QC", c)], pk(sbk))
                        act(PT[pb][:, 0:nq], ps[:, sbk, 0:nq], AF.Exp, pk(sbk), [("PT", pb)])

                    def pvC(hc=hc, mt=mt, pb=pb):
                        for qt in range(nqt):
                            mm(ps[0:qw, 2 + qt, 0:257], PT[pb][:, qt * 128:qt * 128 + qw], vaugC[:, mt, 0:257], mt == 0,
                               mt == 1, [("PT", pb), "vaugC"], pk(2 + qt))
                        if mt == 1:
                            for qt in range(nqt):
                                rc = tcol()
                                fb = next_fb()
                                ov = ps[0:qw, 2 + qt, 0:257]
                                S.op("dve", lambda e, ov=ov, rc=rc: e.reciprocal(cols[0:qw, rc:rc + 1], ov[:, 256:257]),
                                     pk(2 + qt), [("col", rc)])
                                vts(Ob[fb][0:qw, 0:256], ov[:, 0:256], cols[0:qw, rc:rc + 1], None, ALU.mult, None,
                                    pk(2 + qt) + [("col", rc)], [("Ob", fb)])
                                for c in range(2):
                                    tp(psb(TPB, 512 + c * 128, 512 + c * 128 + qw), Ob[fb][0:qw, c * 128:(c + 1) * 128],
                                       ident[0:qw, 0:qw], [("Ob", fb)], pk(TPB))
                                for c in range(2):
                                    vcopy(oT[:, 16 + 2 * hc + c, qt * 128:qt * 128 + qw],
                                          psb(TPB, 512 + c * 128, 512 + c * 128 + qw), pk(TPB), [("oT", 16 + 2 * hc + c)])

                    stepsC.append(dict(pre=[head_preC] if mt == 0 else [], qk=qkC, pv=pvC, nolook=(mt == 0)))
            run_pipeline(stepsC)

            if DBG['stage'] and DBG['stage'] <= 5:
                S.barrier()
                return
            for np_ in range(8):
                for br in range(3):
                    wg, wkg = wload(WMg[np_, br], 4096, ("WMg", np_, br))
                    wbr, wkb = wload(WMb[np_, br], 2048, ("WMb", np_, br))
                    for cc in range(2):
                        n = 2 * np_ + cc
                        bg = nextps([0, 1, 2, 3, 4, 5, 6])
                        bp_ = nextps([0, 1, 2, 3, 4, 5, 6])
                        if bp_ == bg:
                            bp_ = nextps([0, 1, 2, 3, 4, 5, 6])
                        for k in range(16):
                            mm(ps[:, bg, 0:nq], wg[:, k * 256 + cc * 128:k * 256 + (cc + 1) * 128], hT[:, k, 0:nq],
                               k == 0, k == 15, [wkg, ("hT", k)], pk(bg))
                        for k in range(8):
                            mm(ps[:, bp_, 0:nq], wbr[:, k * 256 + cc * 128:k * 256 + (cc + 1) * 128],
                               oT[:, br * 8 + k, 0:nq], k == 0, k == 7, [wkb, ("oT", br * 8 + k)], pk(bp_))
                        si = nextps([0, 1])
                        act(sig[si][:, 0:nq], ps[:, bg, 0:nq], AF.Sigmoid, pk(bg) + [("colblk", C_BG)], [("sig", si)],
                            bias=cols[:, C_BG + br * 16 + n:C_BG + br * 16 + n + 1])
                        acc = (macc, tmpm)[cc]
                        akey = ("macc", cc)
                        if br == 0:
                            vtt(acc[:, 0:nq], sig[si][:, 0:nq], ps[:, bp_, 0:nq], ALU.mult, [("sig", si)] + pk(bp_), [akey])
                        else:
                            vtt(sig[si][:, 0:nq], sig[si][:, 0:nq], ps[:, bp_, 0:nq], ALU.mult, [("sig", si)] + pk(bp_),
                                [("sig", si)])
                            if br == 1:
                                vtt(acc[:, 0:nq], acc[:, 0:nq], sig[si][:, 0:nq], ALU.add, [akey, ("sig", si)], [akey])
                            else:
                                vtt(mT[:, n, 0:nq], acc[:, 0:nq], sig[si][:, 0:nq], ALU.add, [akey, ("sig", si)],
                                    [("mT", n)])

            if DBG['stage'] and DBG['stage'] <= 6:
                S.barrier()
                return
            S.barrier()

            dma("pool", gbc, g_mpost.partition_broadcast(128), [], ["gbc"])
            for nb in range(8):
                wt, wk = wload(WO[nb], 4096, ("WO", nb))
                for tt in range(nqt):
                    b, half = nexthalf()
                    pkey = ("ps", b)
                    for k in range(16):
                        mm(ps[0:qw, b, half:half + 256], mT[:, k, tt * 128:tt * 128 + qw], wt[:, k * 256:(k + 1) * 256], k == 0,
                           k == 15, [wk, ("mT", k)], [pkey])
                    if (nb + tt) % 2 == 0:
                        act(y1t[tt][0:qw, nb * 256:(nb + 1) * 256], ps[0:qw, b, half:half + 256], AF.Identity, [pkey], [("y1", tt, nb)])
                    else:
                        vcopy(y1t[tt][0:qw, nb * 256:(nb + 1) * 256], ps[0:qw, b, half:half + 256], [pkey], [("y1", tt, nb)])
            y1keys = lambda tt: [("y1", tt, nb) for nb in range(8)]
            for tt in range(nqt):
                ssc, rsc = tcol(), tcol()
                act(sqf[0:qw, :], y1t[tt][0:qw, :], AF.Square, y1keys(tt), ["sqf", ("col", ssc)],
                    accum=cols[0:qw, ssc:ssc + 1])
                rstd_from_ss(ssc, D, rsc, qw)
                vstt(y1t[tt][0:qw, :], y1t[tt][0:qw, :], cols[0:qw, rsc:rsc + 1], gbc[0:qw, :], ALU.mult, ALU.mult,
                     y1keys(tt) + [("col", rsc), "gbc"], [("y1", tt, 0)])
                xi = xrr[0] % 2
                xrr[0] += 1
                dma("sp", xt[xi][0:qw, :], xrows(tt), [], ["xt%d" % xi])
                vtt(y1t[tt][0:qw, :], y1t[tt][0:qw, :], xt[xi][0:qw, :], ALU.add, [("y1", tt, 0), "xt%d" % xi],
                    [("y1", tt, 0)])
                dma("pool", x1s[tt, 0:qw, :], y1t[tt][0:qw, :], [("y1", tt, 0)], [("x1s", tt)])
                norm_transpose(None, qw, C_GFPRE, mT, "mT", tt * 128, from_tile=(y1t[tt], ("y1", tt, 0)))

            if DBG['stage'] and DBG['stage'] <= 7:
                S.barrier()
                return
            k0 = 0
            for t3 in range(3):
                nk = THIRDS[t3]
                for pr in range(k0 // 2, (k0 + nk) // 2):
                    wa, wka = wload(WFIa[pr], 4096, ("WFIa", pr))
                    wb, wkb = wload(WFIb[pr], 4096, ("WFIb", pr))
                    for cc in range(2):
                        hc = 2 * pr + cc
                        ba = nextps([0, 1, 2, 3, 4, 5, 6])
                        bb = nextps([0, 1, 2, 3, 4, 5, 6])
                        if bb == ba:
                            bb = nextps([0, 1, 2, 3, 4, 5, 6])
                        for k in range(16):
                            mm(ps[:, ba, 0:nq], wa[:, k * 256 + cc * 128:k * 256 + (cc + 1) * 128], mT[:, k, 0:nq],
                               k == 0, k == 15, [wka, ("mT", k)], pk(ba))
                        for k in range(16):
                            mm(ps[:, bb, 0:nq], wb[:, k * 256 + cc * 128:k * 256 + (cc + 1) * 128], mT[:, k, 0:nq],
                               k == 0, k == 15, [wkb, ("mT", k)], pk(bb))
                        si = nextps([0, 1])
                        act(sa[si][:, 0:nq], ps[:, ba, 0:nq], AF.Silu, pk(ba), [("sa", si)])
                        vtt(uT[:, hc - k0, 0:nq], sa[si][:, 0:nq], ps[:, bb, 0:nq], ALU.mult, [("sa", si)] + pk(bb),
                            [("hT", hc - k0)])
                for nb in range(8):
                    wt, wk = wload(WFO[t3, nb, :, 0:nk * 256], nk * 256, ("WFO", t3, nb))
                    for tt in range(nqt):
                        b, half = nexthalf()
                        pkey = ("ps", b)
                        for kk in range(nk):
                            mm(ps[0:qw, b, half:half + 256], uT[:, kk, tt * 128:tt * 128 + qw], wt[:, kk * 256:(kk + 1) * 256],
                               kk == 0, kk == nk - 1, [wk, ("hT", kk)], [pkey])
                        dst = y1t[tt][0:qw, nb * 256:(nb + 1) * 256]
                        if t3 == 0:
                            act(dst, ps[0:qw, b, half:half + 256], AF.Identity, [pkey],
                                [("y2", tt, nb)] + ([("y1", tt, 0)] if nb == 0 else []))
                        else:
                            vtt(dst, dst, ps[0:qw, b, half:half + 256], ALU.add, [pkey, ("y2", tt, nb)], [("y2", tt, nb)])
                k0 += nk
            dma("pool", gbc, g_fpost.partition_broadcast(128), [], ["gbc"])
            y2keys = lambda tt: [("y2", tt, nb) for nb in range(8)]
            for tt in range(nqt):
                ssc, rsc = tcol(), tcol()
                act(sqf[0:qw, :], y1t[tt][0:qw, :], AF.Square, y2keys(tt), ["sqf", ("col", ssc)],
                    accum=cols[0:qw, ssc:ssc + 1])
                rstd_from_ss(ssc, D, rsc, qw)
                vstt(y1t[tt][0:qw, :], y1t[tt][0:qw, :], cols[0:qw, rsc:rsc + 1], gbc[0:qw, :], ALU.mult, ALU.mult,
                     y2keys(tt) + [("col", rsc), "gbc"], [("y2", tt, 0)])
                xi = xrr[0] % 2
                xrr[0] += 1
                dma("sp", xt[xi][0:qw, :], x1s[tt, 0:qw, :], [("x1s", tt)], ["xt%d" % xi])
                vtt(y1t[tt][0:qw, :], y1t[tt][0:qw, :], xt[xi][0:qw, :], ALU.add, [("y2", tt, 0), "xt%d" % xi],
                    [("y2", tt, 0)])
                ydst = ys[0:qw, :] if sample else yp[s, pos0 + tt * 128:pos0 + tt * 128 + 128, :]
                dma("pool", ydst, y1t[tt][0:qw, :], [("y2", tt, 0)], [("yout", tt)])


        S.barrier()
        if DBG['blocks'] is None:
            for s in range(2):
                memory_kv(s)
                for j in range(4):
                    block(s, j, T)
            block(2, 2, NS)
        else:
            for item in DBG['blocks']:
                if item[0] == 'mem':
                    memory_kv(item[1])
                else:
                    block(*item)
        S.emit()
    return nc


_CACHE = {}


def kernel(**inputs):
    f = lambda a: np.ascontiguousarray(np.asarray(a, dtype=np.float32))
    x_prompt = f(inputs["x_prompt"])
    x_sample = f(inputs["x_sample"])
    consts = host_consts()
    shared = dict(
        w_in=f(inputs["w_in"][0]), w_mem=f(inputs["w_mem_kv"][0]),
        w_br0=f(inputs["w_br_a"][0]), w_br1=f(inputs["w_br_b"][0]), w_br2=f(inputs["w_br_c"][0]),
        w_out=f(inputs["w_out"][0]), w_fi=f(inputs["w_ffn_in"][0]), w_fo=f(inputs["w_ffn_out"][0]),
        g_mpre=f(inputs["norm_mix_pre"][0]).reshape(16, 128), g_mem=f(inputs["norm_mem"][0]).reshape(16, 128),
        g_fpre=f(inputs["norm_ffn_pre"][0]).reshape(16, 128), g_mpost=f(inputs["norm_mix_post"][0]).reshape(1, D),
        g_fpost=f(inputs["norm_ffn_post"][0]).reshape(1, D), b_gate=f(inputs["b_gate"][0]).reshape(48, 128),
        lam0=f(inputs["lambda_q1"]).reshape(1, 64), lam1=f(inputs["lambda_k1"]).reshape(1, 64),
        lam2=f(inputs["lambda_q2"]).reshape(1, 64), lam3=f(inputs["lambda_k2"]).reshape(1, 64),
        subln=f(inputs["subln_a"]).reshape(1, 128),
        rbp=np.ascontiguousarray(np.pad(f(inputs["rel_bias_b"][0]), ((0, 0), (128, 0)), mode="edge")),
    )
    shared.update(consts)
    in_maps = []
    for c in range(NCORES):
        m = dict(shared)
        m["xp"] = x_prompt[2 * c:2 * c + 2]
        m["xs"] = x_sample[c]
        m["cak"] = f(inputs["cache_a_k"][0, c]).reshape(PAST, 1024)
        m["cav"] = f(inputs["cache_a_v"][0, c]).reshape(PAST, 1024)
        m["cbk"] = f(inputs["cache_b_k"][0, c]).reshape(512, 1024)
        m["cbv"] = f(inputs["cache_b_v"][0, c]).reshape(512, 1024)
        m["cmk"] = f(inputs["cache_mem_k"][0, c]).reshape(256, 1024)
        m["cmv"] = f(inputs["cache_mem_v"][0, c]).reshape(256, 1024)
        m["memp"] = f(inputs["mem_prompt"][2 * c:2 * c + 2])
        in_maps.append(m)
    if "nc" not in _CACHE:
        _CACHE["nc"] = build()
    res = run_bass_kernel_spmd(_CACHE["nc"], in_maps, core_ids=list(range(NCORES)))
    R = res.results
    cat = lambda k: np.concatenate([np.asarray(r[k]) for r in R], axis=0)
    stk = lambda k: np.stack([np.asarray(r[k]) for r in R], axis=0)
    y_prompt = cat("yp")
    y_sample = stk("ys")
    return (
        y_prompt.astype(np.float32), y_sample.astype(np.float32),
        cat("akp").reshape(1, 16, SEQ, 8, 128), cat("avp").reshape(1, 16, SEQ, 8, 128),
        cat("bkp").reshape(1, 16, 512, 8, 128), cat("bvp").reshape(1, 16, 512, 8, 128),
        cat("mkp").reshape(1, 16, 256, 4, 256), cat("mvp").reshape(1, 16, 256, 4, 256),
        stk("aks").reshape(1, 8, NS, 8, 128), stk("avs").reshape(1, 8, NS, 8, 128),
        stk("bks").reshape(1, 8, NS, 8, 128), stk("bvs").reshape(1, 8, NS, 8, 128),
    )
```
